# Optimizing a Trainium2 kernel written in Bass

```python
import math
import jax
import jax.numpy as jnp
from jax import lax
import numpy as np

D_MODEL = 2048
BATCH = 2
SEQ = 4096
DEPTH = 4
DEC_BATCH = 8
DEC_SEQ = 8
PAST_LEN = 16384
PAGE_SIZE = 128

HEAD_DIM = 128
MEM_HEADS = 4
MEM_WIDTH = MEM_HEADS * HEAD_DIM
N_MEM = 256
MIX_WIDTH = D_MODEL - MEM_WIDTH
N_A_LAYERS = DEPTH // 2
N_B_LAYERS = DEPTH - N_A_LAYERS
CONV_CH = MIX_WIDTH
CONV_WIDTH = 31
B_GROUPS = ((128, 1), (512, 4), (2048, 16))
N_GROUPS = len(B_GROUPS)
B_HEADS_PER_GROUP = MIX_WIDTH // (N_GROUPS * HEAD_DIM)
B_HEADS = N_GROUPS * B_HEADS_PER_GROUP
B_OUT = B_HEADS_PER_GROUP * HEAD_DIM
REL_BUCKETS = 32
REL_MAX_DIST = max(w for w, _ in B_GROUPS)
Q_BLOCK = 128
LN_EPS = 1e-5
ALPHA = (2 * DEPTH) ** 0.25
BETA = (8 * DEPTH) ** -0.25
NEG_INF = -1e30
A_IN_WIDTH = 3 * CONV_CH + 2 * MEM_WIDTH
B_IN_WIDTH = B_HEADS * HEAD_DIM + B_OUT + 2 * MEM_WIDTH

kernel_name = 'yoco_conformer_dilated_attn_decoder_step'


def _layernorm(x, g, b):
    xf = x.astype(jnp.float32)
    mu = jnp.mean(xf, -1, keepdims=True)
    var = jnp.mean(jnp.square(xf - mu), -1, keepdims=True)
    y = (xf - mu) * lax.rsqrt(var + LN_EPS)
    return (y * g.astype(jnp.float32) + b.astype(jnp.float32)).astype(x.dtype)


def _t5_bucket(dist):
    max_exact = REL_BUCKETS // 2
    safe = jnp.maximum(dist, 1).astype(jnp.float32)
    large = max_exact + (jnp.log(safe / max_exact) / math.log(REL_MAX_DIST / max_exact)
                         * (REL_BUCKETS - max_exact)).astype(jnp.int32)
    large = jnp.minimum(large, REL_BUCKETS - 1)
    return jnp.where(dist < max_exact, dist, large)


def _group_biases(rel_bias):
    out = []
    for g, (w, d) in enumerate(B_GROUPS):
        dist = d * jnp.arange(w // d + 1, dtype=jnp.int32)
        tab = rel_bias[_t5_bucket(dist)]
        out.append(tab[:, g * B_HEADS_PER_GROUP:(g + 1) * B_HEADS_PER_GROUP].T.astype(jnp.float32))
    return out


def _dilated_mixture(qs, kvs, q_idxs, biases):
    ms, ss, nums = [], [], []
    for g, (w, d) in enumerate(B_GROUPS):
        j = jnp.arange(w // d + 1, dtype=jnp.int32)
        idx = q_idxs[g][:, None] - d * j[None, :]
        valid = idx >= 0
        kv = kvs[g][:, jnp.maximum(idx, 0)]
        k = kv[:, :, :, 0].astype(jnp.float32)
        v = kv[:, :, :, 1].astype(jnp.float32)
        logits = jnp.einsum('bqhd,bqjhd->bqhj', qs[g].astype(jnp.float32), k) * (HEAD_DIM ** -0.5) + biases[g]
        logits = jnp.where(valid[None, :, None, :], logits, NEG_INF)
        m = jnp.max(logits, -1, keepdims=True)
        e = jnp.exp(logits - m)
        ms.append(m[..., 0])
        ss.append(jnp.sum(e, -1))
        nums.append(jnp.einsum('bqhj,bqjhd->bqhd', e, v))
    m_all = jnp.max(jnp.stack(ms), 0)
    coef = [jnp.exp(m - m_all) for m in ms]
    num = coef[0][..., None] * nums[0]
    den = coef[0] * ss[0]
    for g in range(1, N_GROUPS):
        num = num + coef[g][..., None] * nums[g]
        den = den + coef[g] * ss[g]
    return num / den[..., None]


def _prompt_attend(kvs, biases, seq_len):
    def attend(qs):
        def block(start):
            q_idx = start + jnp.arange(Q_BLOCK, dtype=jnp.int32)
            qb = [lax.dynamic_slice_in_dim(q, start, Q_BLOCK, axis=1) for q in qs]
            return _dilated_mixture(qb, kvs, [q_idx] * N_GROUPS, biases)
        starts = jnp.arange(seq_len // Q_BLOCK, dtype=jnp.int32) * Q_BLOCK
        out = lax.map(block, starts)
        return jnp.moveaxis(out, 0, 1).reshape(out.shape[1], seq_len, B_HEADS_PER_GROUP, HEAD_DIM)
    return attend


def _sample_attend(kvs, biases, n_new):
    q_idxs = [kv.shape[1] - n_new + jnp.arange(n_new, dtype=jnp.int32) for kv in kvs]
    def attend(qs):
        return _dilated_mixture(qs, kvs, q_idxs, biases)
    return attend


def _mem_attn(qm, mem_kv):
    b, t, _ = qm.shape
    q = qm.reshape(b, t, MEM_HEADS, HEAD_DIM).astype(jnp.float32)
    k = mem_kv[:, :, 0].astype(jnp.float32)
    v = mem_kv[:, :, 1].astype(jnp.float32)
    p = jax.nn.softmax(jnp.einsum('bthd,bnhd->bhtn', q, k) * (HEAD_DIM ** -0.5), axis=-1)
    return jnp.einsum('bhtn,bnhd->bthd', p, v).reshape(b, t, MEM_WIDTH).astype(qm.dtype)


def _a_layer(x, conv_prev, mem_kv, w_in, w_dw, b_dw, cn_g, cn_b, w_out, ln_g, ln_b):
    h = jnp.einsum('btd,de->bte', x, w_in)
    ga, gb, gate, qm, gm = jnp.split(h, [CONV_CH, 2 * CONV_CH, 3 * CONV_CH, 3 * CONV_CH + MEM_WIDTH], axis=-1)
    u = ga * jax.nn.sigmoid(gb)
    ext = jnp.concatenate([conv_prev.astype(u.dtype), u], axis=1)
    conv = lax.conv_general_dilated(ext, w_dw[:, None, :].astype(ext.dtype), (1,), 'VALID',
                                    dimension_numbers=('NWC', 'WIO', 'NWC'),
                                    feature_group_count=CONV_CH) + b_dw
    c = jax.nn.silu(_layernorm(conv, cn_g, cn_b)) * jax.nn.silu(gate)
    m = _mem_attn(qm, mem_kv) * jax.nn.silu(gm)
    y = jnp.einsum('bte,ed->btd', jnp.concatenate([c, m], axis=-1), w_out)
    new_conv = ext[:, ext.shape[1] - (CONV_WIDTH - 1):]
    return _layernorm(ALPHA * x + y, ln_g, ln_b), new_conv


def _b_layer(x, attend, mem_kv, w_in, w_out, ln_g, ln_b):
    b, t, _ = x.shape
    h = jnp.einsum('btd,de->bte', x, w_in)
    q, gate, qm, gm = jnp.split(h, [MIX_WIDTH, MIX_WIDTH + B_OUT, MIX_WIDTH + B_OUT + MEM_WIDTH], axis=-1)
    q = q.reshape(b, t, N_GROUPS, B_HEADS_PER_GROUP, HEAD_DIM)
    o = attend([q[:, :, g] for g in range(N_GROUPS)])
    o = o.astype(x.dtype).reshape(b, t, B_OUT) * jax.nn.silu(gate)
    m = _mem_attn(qm, mem_kv) * jax.nn.silu(gm)
    y = jnp.einsum('bte,ed->btd', jnp.concatenate([o, m], axis=-1), w_out)
    return _layernorm(ALPHA * x + y, ln_g, ln_b)


def _trunk(x, conv_prev, mem_kv, kv_bufs, a_w_in, a_w_dw, a_b_dw, a_cn_g, a_cn_b, a_w_out,
           b_w_in, b_w_out, w_kv_shared, rel_bias, ln_g, ln_b):
    b, t, _ = x.shape
    biases = _group_biases(rel_bias)
    new_conv = []
    new_bufs = []
    attend = None
    for l in range(DEPTH):
        if l < N_A_LAYERS:
            x, c = _a_layer(x, conv_prev[l], mem_kv[l], a_w_in[l], a_w_dw[l], a_b_dw[l], a_cn_g[l], a_cn_b[l],
                            a_w_out[l], ln_g[l], ln_b[l])
            new_conv.append(c)
        else:
            if l == N_A_LAYERS:
                kv = jnp.einsum('btd,de->bte', x, w_kv_shared).reshape(b, t, 2, N_GROUPS, B_HEADS_PER_GROUP, HEAD_DIM)
                kv_new = [kv[:, :, :, g] for g in range(N_GROUPS)]
                if kv_bufs is None:
                    new_bufs = [k[:, t - min(w, t):] for k, (w, _) in zip(kv_new, B_GROUPS)]
                    attend = _prompt_attend(kv_new, biases, t)
                else:
                    exts = [jnp.concatenate([buf.astype(k.dtype), k], axis=1) for buf, k in zip(kv_bufs, kv_new)]
                    new_bufs = [e[:, e.shape[1] - buf.shape[1]:] for e, buf in zip(exts, kv_bufs)]
                    attend = _sample_attend(exts, biases, t)
            i = l - N_A_LAYERS
            x = _b_layer(x, attend, mem_kv[l], b_w_in[i], b_w_out[i], ln_g[l], ln_b[l])
    return x, jnp.stack(new_conv), new_bufs


def setup_inputs(seed: int = 0) -> dict:
    key = jax.random.key(seed)
    ks = iter(jax.random.split(key, 32))
    f32 = jnp.float32

    def nrm(shape, scale):
        return jax.random.normal(next(ks), shape, f32) * scale

    kv_scale = jnp.array([1.0, BETA], f32)
    inp = {}
    inp['x_prompt'] = nrm((BATCH, SEQ, D_MODEL), 1.0)
    inp['x_sample'] = nrm((DEC_BATCH, DEC_SEQ, D_MODEL), 1.0)
    inp['state_conv'] = nrm((N_A_LAYERS, DEC_BATCH, CONV_WIDTH - 1, CONV_CH), 0.5)
    for g, (w, _) in enumerate(B_GROUPS):
        inp['state_kv_g%d' % g] = nrm((DEC_BATCH, min(w, PAST_LEN), 2, B_HEADS_PER_GROUP, HEAD_DIM), 1.0) * kv_scale[None, None, :, None, None]
    inp['cache_mem_kv'] = nrm((DEPTH, DEC_BATCH, N_MEM, 2, MEM_HEADS, HEAD_DIM), 1.0) * kv_scale[None, None, None, :, None, None]
    inp['mem_prompt'] = nrm((BATCH, N_MEM, D_MODEL), 1.0)
    inp['a_w_in'] = nrm((N_A_LAYERS, D_MODEL, A_IN_WIDTH), D_MODEL ** -0.5)
    inp['a_w_dw'] = nrm((N_A_LAYERS, CONV_WIDTH, CONV_CH), CONV_WIDTH ** -0.5)
    inp['a_b_dw'] = nrm((N_A_LAYERS, CONV_CH), 0.01)
    inp['a_cn_g'] = 1.0 + nrm((N_A_LAYERS, CONV_CH), 0.01)
    inp['a_cn_b'] = nrm((N_A_LAYERS, CONV_CH), 0.01)
    inp['a_w_out'] = nrm((N_A_LAYERS, CONV_CH + MEM_WIDTH, D_MODEL), (CONV_CH + MEM_WIDTH) ** -0.5 * BETA)
    inp['b_w_in'] = nrm((N_B_LAYERS, D_MODEL, B_IN_WIDTH), D_MODEL ** -0.5)
    inp['b_w_out'] = nrm((N_B_LAYERS, B_OUT + MEM_WIDTH, D_MODEL), (B_OUT + MEM_WIDTH) ** -0.5 * BETA)
    inp['w_kv_shared'] = (nrm((D_MODEL, 2, B_HEADS * HEAD_DIM), D_MODEL ** -0.5) * kv_scale[None, :, None]).reshape(D_MODEL, 2 * B_HEADS * HEAD_DIM)
    inp['w_mem_kv'] = (nrm((DEPTH, D_MODEL, 2, MEM_WIDTH), D_MODEL ** -0.5) * kv_scale[None, None, :, None]).reshape(DEPTH, D_MODEL, 2 * MEM_WIDTH)
    inp['rel_bias'] = nrm((REL_BUCKETS, B_HEADS), 0.5)
    inp['ln_g'] = 1.0 + nrm((DEPTH, D_MODEL), 0.01)
    inp['ln_b'] = nrm((DEPTH, D_MODEL), 0.01)
    return inp


def reference(x_prompt, x_sample, state_conv, state_kv_g0, state_kv_g1, state_kv_g2, cache_mem_kv, mem_prompt,
              a_w_in, a_w_dw, a_b_dw, a_cn_g, a_cn_b, a_w_out, b_w_in, b_w_out, w_kv_shared, w_mem_kv,
              rel_bias, ln_g, ln_b):
    bp = x_prompt.shape[0]
    new_mem_kv_prompt = jnp.einsum('bnd,lde->lbne', mem_prompt, w_mem_kv).reshape(DEPTH, bp, N_MEM, 2, MEM_HEADS, HEAD_DIM)
    conv_zero = jnp.zeros((N_A_LAYERS, bp, CONV_WIDTH - 1, CONV_CH), x_prompt.dtype)
    y_prompt, conv_p, bufs_p = _trunk(x_prompt, conv_zero, new_mem_kv_prompt, None,
                                      a_w_in, a_w_dw, a_b_dw, a_cn_g, a_cn_b, a_w_out,
                                      b_w_in, b_w_out, w_kv_shared, rel_bias, ln_g, ln_b)
    y_sample, conv_s, bufs_s = _trunk(x_sample, state_conv, cache_mem_kv, [state_kv_g0, state_kv_g1, state_kv_g2],
                                      a_w_in, a_w_dw, a_b_dw, a_cn_g, a_cn_b, a_w_out,
                                      b_w_in, b_w_out, w_kv_shared, rel_bias, ln_g, ln_b)
    return (y_prompt, y_sample, conv_p, conv_s, bufs_p[0], bufs_s[0], bufs_p[1], bufs_s[1], bufs_p[2], bufs_s[2], new_mem_kv_prompt)
```

```python
import numpy as np
from contextlib import ExitStack
import concourse.bass as bass
import concourse.mybir as mybir
from concourse.bass_utils import run_bass_kernel_spmd

F32 = mybir.dt.float32
BF16 = mybir.dt.bfloat16
I32 = mybir.dt.int32
AF = mybir.ActivationFunctionType
ALU = mybir.AluOpType

D = 2048
DEPTH = 4
SEQ = 4096
NCORE = 8
CH = 1024
HALO = 64
NS = 8
NTA = HALO + CH + NS
NTB = CH + NS
CC = 1536
NJ = 12
KW = 31
A_IN = 5632
B_IN = 3072
NMEM = 256
LN_EPS = 1e-5
ALPHA = (2 * DEPTH) ** 0.25
INV_ALPHA = 1.0 / ALPHA
EPS_Z = LN_EPS / (ALPHA * ALPHA)
SCALE = 128 ** -0.5
GROUPS = ((128, 1), (512, 4), (2048, 16))

ENGS = ("tensor", "vector", "scalar", "gpsimd", "sync")
N_DMA_SEMS = 12


class Op:
    __slots__ = ("eng", "fn", "dma", "deps", "signal", "sigval", "sem", "idx")

    def __init__(self, eng, fn, dma):
        self.eng = eng
        self.fn = fn
        self.dma = dma
        self.deps = []
        self.signal = False
        self.sigval = 0
        self.sem = None
        self.idx = 0


class _St:
    __slots__ = ("w", "r")

    def __init__(self):
        self.w = None
        self.r = {}


class Prog:
    def __init__(self):
        self.ops = {e: [] for e in ENGS}
        self.res = {}
        self.final_dmas = []

    def op(self, eng, fn, reads=(), writes=(), dma=False, final=False):
        o = Op(eng, fn, dma)
        o.idx = len(self.ops[eng])
        self.ops[eng].append(o)
        deps = {}
        for r in reads:
            st = self.res.get(r)
            if st is not None and st.w is not None:
                deps[id(st.w)] = st.w
        for w in writes:
            st = self.res.get(w)
            if st is not None:
                if st.w is not None:
                    deps[id(st.w)] = st.w
                for q in st.r.values():
                    deps[id(q)] = q
        for q in deps.values():
            if q is o:
                continue
            if (not q.dma) and (not dma) and q.eng == eng and eng == "tensor":
                continue
            q.signal = True
            o.deps.append(q)
        for r in reads:
            st = self.res.setdefault(r, _St())
            st.r[("dma", id(o)) if dma else eng] = o
        for w in writes:
            st = self.res.setdefault(w, _St())
            st.w = o
            st.r = {}
        if final:
            o.signal = True
            self.final_dmas.append(o)
        return o

    def emit(self, nc, stack):
        sems = {}
        for e in ("tensor", "vector", "scalar", "gpsimd"):
            sems[e] = stack.enter_context(nc.semaphore("c_" + e))
        dsems = {}
        for e in ("sync", "gpsimd", "scalar"):
            dsems[e] = [stack.enter_context(nc.semaphore("d_%s%d" % (e, i))) for i in range(N_DMA_SEMS)]
        for e in ENGS:
            cnt = 0
            dcnt = 0
            dvals = [0] * N_DMA_SEMS
            for o in self.ops[e]:
                if o.dma:
                    k = dcnt % N_DMA_SEMS
                    dcnt += 1
                    dvals[k] += 16
                    o.sem = dsems[e][k]
                    o.sigval = dvals[k]
                elif o.signal:
                    cnt += 1
                    o.sem = sems[e]
                    o.sigval = cnt
        block = stack.enter_context(nc.Block())
        prog = self

        def run(e):
            def body(eng):
                waited = {}
                last_on_sem = {}
                for o in prog.ops[e]:
                    if o.dma:
                        prev = last_on_sem.get(id(o.sem))
                        if prev is not None and waited.get(id(o.sem), 0) < prev:
                            eng.wait_ge(o.sem, prev)
                            waited[id(o.sem)] = prev
                    for q in o.deps:
                        k = id(q.sem)
                        if waited.get(k, 0) < q.sigval:
                            eng.wait_ge(q.sem, q.sigval)
                            waited[k] = q.sigval
                    ins = o.fn(eng)
                    if o.dma:
                        ins.then_inc(o.sem, 16)
                        last_on_sem[id(o.sem)] = o.sigval
                    elif o.signal:
                        ins.then_inc(o.sem, 1)
                if e == "sync":
                    for o in prog.final_dmas:
                        k = id(o.sem)
                        if waited.get(k, 0) < o.sigval:
                            eng.wait_ge(o.sem, o.sigval)
                            waited[k] = o.sigval
            return body

        for e in ENGS:
            if self.ops[e] or e == "sync":
                getattr(block, e)(run(e))


class Builder:
    def __init__(self, phase):
        self.phase = phase
        self.nc = bass.Bass("TRN2", target_bir_lowering=False)
        self.p = Prog()
        self.st = ExitStack()
        self.nbank = 0
        self.nscr = 0
        self.ndg = 0
        self.items = []

    def din(self, name, shape, dt=F32):
        return self.nc.dram_tensor(name, list(shape), dt, kind="ExternalInput").ap()

    def dout(self, name, shape, dt=F32):
        return self.nc.dram_tensor(name, list(shape), dt, kind="ExternalOutput").ap()

    def sb(self, name, shape, dt):
        return self.st.enter_context(self.nc.sbuf_tensor("s_" + name, list(shape), dt))

    def mm(self, out, lhsT, rhs, start, stop, reads, writes):
        self.p.op("tensor", lambda e: e.matmul(out, lhsT, rhs, start=start, stop=stop), reads, writes)

    def act(self, out, in_, func, reads, writes, bias=None, scale=None):
        kw = {}
        if bias is not None:
            kw["bias"] = bias
        if scale is not None:
            kw["scale"] = scale
        self.p.op("scalar", lambda e: e.activation(out=out, in_=in_, func=func, **kw), reads, writes)

    def tt(self, out, a, b, op, reads, writes, eng="vector"):
        self.p.op(eng, lambda e: e.tensor_tensor(out, a, b, op), reads, writes)

    def ts(self, out, a, s1, s2, op0, op1, reads, writes, eng="vector"):
        if s2 is None:
            self.p.op(eng, lambda e: e.tensor_scalar(out, a, s1, None, op0), reads, writes)
        else:
            self.p.op(eng, lambda e: e.tensor_scalar(out, a, s1, s2, op0, op1), reads, writes)

    def stt(self, out, in0, scalar, in1, op0, op1, reads, writes, eng="vector"):
        self.p.op(eng, lambda e: e.scalar_tensor_tensor(out, in0, scalar, in1, op0, op1), reads, writes)

    def cp(self, out, a, reads, writes, eng="vector"):
        self.p.op(eng, lambda e: e.tensor_copy(out, a), reads, writes)

    def dma(self, out, in_, reads, writes, eng="sync", final=False):
        self.p.op(eng, lambda e: e.dma_start(out=out, in_=in_), reads, writes, dma=True, final=final)

    def bank(self):
        i = self.nbank % 8
        self.nbank += 1
        return self.ps[i], ("ps", i)

    def scr(self):
        i = self.nscr % self.NSCR
        self.nscr += 1
        return i, ("scr", i)

    def scr_f(self, i):
        return self.scrt[:, i, :]

    def scr_b(self, i):
        return self.scrt[:, i, :].bitcast(BF16)

    def add_item(self, w2d, kchunks, groups, compute):
        self.items.append((w2d, kchunks, groups, compute))

    def flush_items(self):
        NSL = self.NSLAB
        n = len(self.items)

        def load(i):
            w2d, kch, groups, _ = self.items[i]
            s = i % NSL
            off = 0
            for (c0, ncol) in groups:
                src = w2d[:, c0:c0 + ncol].rearrange("(kc p) c -> p kc c", p=128)
                self.dma(self.slab[s][:, 0:kch, off:off + ncol], src, [], [("slab", s)], eng="gpsimd")
                off += ncol
        for i in range(min(NSL - 1, n)):
            load(i)
        for i in range(n):
            if i + NSL - 1 < n:
                load(i + NSL - 1)
            self.items[i][3](self.slab[i % NSL], ("slab", i % NSL))
        self.items = []


def _common_decl(b, ntok, nslab, npar, own_mpart):
    b.ps = [b.st.enter_context(b.nc.psum_tensor("ps%d" % i, [128, 512], F32)) for i in range(8)]
    b.NSCR = 10
    b.scrt = b.sb("scrt", [128, b.NSCR, 512], F32)
    b.NSLAB = nslab
    b.slab = [b.sb("slab%d" % i, [128, 16, 256], BF16) for i in range(b.NSLAB)]
    b.xb = b.sb("xb", [128, 16, ntok], BF16)
    b.xlo = b.sb("xlo", [128, 16, ntok], BF16)
    b.ident = b.sb("ident", [128, 128], BF16)
    b.ones = b.sb("ones", [128, 128], BF16)
    b.rstd = b.sb("rstd", [128, ntok], F32)
    b.shift = b.sb("shift", [128, ntok], F32)
    b.lnp = b.sb("lnp", [128, 4, 2, 16], F32)
    b.mkT = [b.sb("mkT%d" % i, [128, 4, NMEM], BF16) for i in range(npar)]
    b.mv = [b.sb("mv%d" % i, [128, 2, 512], BF16) for i in range(npar)]
    b.ckT = [b.sb("ckT%d" % i, [128, 4, NMEM], BF16) for i in range(npar)]
    b.cv_ = [b.sb("cvv%d" % i, [128, 2, 512], BF16) for i in range(npar)]
    if own_mpart:
        b.mpart = b.sb("mpart", [128, 4, ntok], BF16)


def _load_x(b, xT, cts):
    for m in range(16):
        for ci, (c0, n) in enumerate(cts):
            si, sr = b.scr()
            t = b.scr_f(si)[:, 0:n]
            b.dma(t, xT[:, m, c0:c0 + n], [], [sr])
            b.act(b.xb[:, m, c0:c0 + n], t, AF.Copy, [sr], [("xb", m, ci)])
            b.tt(b.xlo[:, m, c0:c0 + n], t, b.xb[:, m, c0:c0 + n], ALU.subtract, [sr, ("xb", m, ci)], [("xlo", m, ci)])


def _stats(b, srcs, cts, nfeat, eps):
    inv = 1.0 / nfeat
    for ci, (c0, n) in enumerate(cts):
        lst = srcs(ci)
        sb_, sr_ = b.bank()
        qb_, qr_ = b.bank()
        nl = len(lst)
        for k, (ap, rn) in enumerate(lst):
            b.mm(sb_[:, 0:n], b.ones[:], ap, k == 0, k == nl - 1, [rn, "ones"], [sr_])
            si, sr = b.scr()
            sq = b.scr_b(si)[:, 0:n]
            b.act(sq, ap, AF.Square, [rn], [sr])
            b.mm(qb_[:, 0:n], b.ones[:], sq, k == 0, k == nl - 1, [sr, "ones"], [qr_])
        si, sr = b.scr()
        mean = b.scr_f(si)[:, 0:n]
        b.ts(mean, sb_[:, 0:n], inv, None, ALU.mult, None, [sr_], [sr])
        si2, sr2 = b.scr()
        msq = b.scr_f(si2)[:, 0:n]
        b.tt(msq, mean, mean, ALU.mult, [sr], [sr2])
        b.stt(msq, qb_[:, 0:n], inv, msq, ALU.mult, ALU.subtract, [qr_, sr2], [sr2])
        b.ts(msq, msq, eps, None, ALU.add, None, [sr2], [sr2])
        b.act(msq, msq, AF.Sqrt, [sr2], [sr2])
        b.p.op("vector", lambda e, o=b.rstd[:, c0:c0 + n], i=msq: e.reciprocal(o, i), [sr2], [("rstd", ci)])
        b.stt(b.shift[:, c0:c0 + n], mean, -1.0, b.rstd[:, c0:c0 + n], ALU.mult, ALU.mult, [sr, ("rstd", ci)], [("shift", ci)])


def _mem_kv(b, l, par, memT, w_mem_kv, memkv_o):
    w2d = w_mem_kv[l]
    for half in range(4):

        def comp(slab, sres, half=half):
            c0 = half * 256
            for tk in range(2):
                bk, br = b.bank()
                for kc in range(16):
                    b.mm(bk[:, 0:256], memT[:, kc, tk * 128:(tk + 1) * 128], slab[:, kc, 0:256], kc == 0, kc == 15,
                         ["memT", sres], [br])
                si, sr = b.scr()
                t = b.scr_f(si)[:, 0:256]
                b.act(t, bk[:, 0:256], AF.Copy, [br], [sr])
                if memkv_o is not None:
                    b.dma(memkv_o[l, tk * 128:(tk + 1) * 128, c0:c0 + 256], t, [sr], [], final=True)
                if half >= 2:
                    b.cp(b.mv[par][:, tk, c0 - 512:c0 - 512 + 256], t, [sr], [("mv", par)])
            if half < 2:
                for hh in range(2):
                    h = half * 2 + hh
                    bk, br = b.bank()
                    for kc in range(16):
                        b.mm(bk[:, 0:NMEM], slab[:, kc, hh * 128:(hh + 1) * 128], memT[:, kc, :], kc == 0, kc == 15,
                             ["memT", sres], [br])
                    b.act(b.mkT[par][:, h, :], bk[:, 0:NMEM], AF.Copy, [br], [("mkT", par)])
        b.add_item(w2d, 16, [(half * 256, 256)], comp)


def _mem_attn(b, par, h, qT, qres, sg, sgres, cts, ns_c0, ci, c0, n):
    segs = []
    pe = min(c0 + n, ns_c0)
    if pe > c0:
        segs.append((0, pe - c0, b.mkT[par], b.mv[par], ("mkT", par), ("mv", par)))
    if c0 + n > ns_c0:
        s0 = max(c0, ns_c0) - c0
        segs.append((s0, n - s0, b.ckT[par], b.cv_[par], ("ckT", par), ("cvv", par)))
    for (o, w, KT, V, kres, vres) in segs:
        pts = []
        for kc in range(2):
            bk, br = b.bank()
            b.mm(bk[:, 0:w], KT[:, h, kc * 128:(kc + 1) * 128], qT[:, o:o + w], True, True, [kres, qres], [br])
            si, sr = b.scr()
            pt = b.scr_b(si)[:, 0:w]
            b.act(pt, bk[:, 0:w], AF.Exp, [br], [sr], scale=SCALE)
            pts.append((pt, sr))
        nb_, nr_ = b.bank()
        db_, dr_ = b.bank()
        for kc in range(2):
            b.mm(nb_[:, 0:w], V[:, kc, h * 128:(h + 1) * 128], pts[kc][0], kc == 0, kc == 1, [vres, pts[kc][1]], [nr_])
        for kc in range(2):
            b.mm(db_[:, 0:w], b.ones[:], pts[kc][0], kc == 0, kc == 1, ["ones", pts[kc][1]], [dr_])
        si, sr = b.scr()
        rd = b.scr_f(si)[:, 0:w]
        b.p.op("vector", lambda e, rd=rd, db_=db_, w=w: e.reciprocal(rd, db_[:, 0:w]), [dr_], [sr])
        b.tt(rd, nb_[:, 0:w], rd, ALU.mult, [nr_, sr], [sr])
        b.tt(b.mpart[:, h, c0 + o:c0 + o + w], rd, sg[:, o:o + w], ALU.mult, [sr, sgres], [("mp", h, ci)])


def _out_ln(b, l, w_out2d, nk, cm_src, cts, last_out=None):
    for mq in range(8):

        def comp(slab, sres, mq=mq):
            for mo in range(2):
                m = mq * 2 + mo
                for ci, (c0, n) in enumerate(cts):
                    bk, br = b.bank()
                    for kc in range(nk):
                        ap, rn = cm_src(kc, ci, c0, n)
                        b.mm(bk[:, 0:n], slab[:, kc, mo * 128:(mo + 1) * 128], ap, kc == 0, False, [sres, rn], [br])
                    b.mm(bk[:, 0:n], b.ident[:], b.xb[:, m, c0:c0 + n], False, False, ["ident", ("xb", m, ci)], [br])
                    b.mm(bk[:, 0:n], b.ident[:], b.xlo[:, m, c0:c0 + n], False, True, ["ident", ("xlo", m, ci)], [br])
                    b.act(b.xb[:, m, c0:c0 + n], bk[:, 0:n], AF.Copy, [br], [("xb", m, ci)])
                    b.tt(b.xlo[:, m, c0:c0 + n], bk[:, 0:n], b.xb[:, m, c0:c0 + n], ALU.subtract,
                         [br, ("xb", m, ci)], [("xlo", m, ci)])
        b.add_item(w_out2d, nk, [(mq * 256, 256)], comp)
    b.flush_items()
    _stats(b, lambda ci: [(b.xb[:, m, cts[ci][0]:cts[ci][0] + cts[ci][1]], ("xb", m, ci)) for m in range(16)], cts, D, EPS_Z)
    for m in range(16):
        for ci, (c0, n) in enumerate(cts):
            si, sr = b.scr()
            s = b.scr_f(si)[:, 0:n]
            b.tt(s, b.xb[:, m, c0:c0 + n], b.xlo[:, m, c0:c0 + n], ALU.add, [("xb", m, ci), ("xlo", m, ci)], [sr])
            b.tt(s, s, b.rstd[:, c0:c0 + n], ALU.mult, [sr, ("rstd", ci)], [sr])
            b.tt(s, s, b.shift[:, c0:c0 + n], ALU.add, [sr, ("shift", ci)], [sr])
            b.act(s, s, AF.Identity, [sr, "lnp"], [sr], bias=b.lnp[:, l, 1, m:m + 1], scale=b.lnp[:, l, 0, m:m + 1])
            if last_out is not None:
                b.dma(last_out[:, m, c0:c0 + n], s, [sr], [], final=True)
            b.cp(b.xb[:, m, c0:c0 + n], s, [sr], [("xb", m, ci)], eng="gpsimd")
            b.tt(b.xlo[:, m, c0:c0 + n], s, b.xb[:, m, c0:c0 + n], ALU.subtract, [sr, ("xb", m, ci)], [("xlo", m, ci)])


CTA = [(0, 512), (512, 512), (1024, 72)]
CONV_LO = 32


def build_A():
    b = Builder("A")
    nc = b.nc
    xT = b.din("xT", [128, 16, NTA])
    memT_d = b.din("memT", [128, 16, NMEM])
    ident_d = b.din("identf", [128, 128])
    hmask_d = b.din("hmask", [128, 1])
    a_w_in = b.din("a_w_in", [2, D, A_IN])
    a_w_out = b.din("a_w_out", [2, D, D])
    w_kv = b.din("w_kv", [D, 3072])
    w_mem_kv = b.din("w_mem_kv", [4, D, 1024])
    wdw_d = b.din("wdw", [128, 2, NJ, KW])
    cnp_d = b.din("cnp", [128, 2, 3, NJ])
    lnp_d = b.din("lnp", [128, 4, 2, 16])
    convst_d = b.din("convst", [128, 2, NJ, 30])
    convst_nat = b.din("convst_nat", [2, 30, CC])
    ckT_d = b.din("ckT", [128, 4, 4, NMEM])
    cv_d = b.din("cv", [128, 4, 2, 512])

    memkv_o = b.dout("memkv_o", [4, NMEM, 1024])
    convp_o = b.dout("convp_o", [128, 2, NJ, 30])
    convsn_o = b.dout("convsn_o", [128, 2, NJ, NS])
    convss_o = b.dout("convss_o", [2, 22, CC])
    kT_o = b.dout("kT_o", [128, 12, NTB])
    v_o = b.dout("v_o", [NTB, CC])
    x2_o = b.dout("x2_o", [128, 16, NTB])
    ks_o = b.dout("ks_o", [NS, CC])
    skv_d = [b.din("skv%d" % g, [GROUPS[g][0], 1024]) for g in range(3)]
    skv_o = [b.dout("skv_o%d" % g, [GROUPS[g][0] - NS, 1024]) for g in range(3)]

    _common_decl(b, NTA, 3, 2, True)
    ring = b.sb("ring", [128, NJ + 1, NTA], BF16)
    exts = b.sb("exts", [128, NJ, 30 + NS], BF16)
    memT = b.sb("memTs", [128, 16, NMEM], BF16)
    hmask = b.sb("hmask_s", [128, 1], F32)
    wdw = b.sb("wdw_s", [128, 2, NJ, KW], F32)
    cnp = b.sb("cnp_s", [128, 2, 3, NJ], F32)
    NDG = 40
    dg = b.sb("dg", [128, NDG, 128], BF16)
    convo = b.sb("convo", [128, 2, NJ, 30], F32)
    convsn = b.sb("convsn", [128, 2, NJ, NS], F32)

    b.dma(b.ident[:], ident_d, [], ["ident"], eng="gpsimd")
    b.p.op("vector", lambda e: e.memset(b.ones[:], 1.0), [], ["ones"])
    b.p.op("vector", lambda e: e.memset(ring[:], 0.0), [], [("ring", s, ci) for s in range(NJ + 1) for ci in range(3)])
    b.p.op("vector", lambda e: e.memset(b.mpart[:], 0.0), [], [("mp", h, ci) for h in range(4) for ci in range(3)])
    b.dma(hmask[:], hmask_d, [], ["hmask"])
    b.dma(wdw[:], wdw_d, [], ["wdw"])
    b.dma(cnp[:], cnp_d, [], ["cnp"])
    b.dma(b.lnp[:], lnp_d, [], ["lnp"])
    b.dma(memT[:], memT_d, [], ["memT"], eng="gpsimd")
    for l in range(2):
        b.dma(convss_o[l], convst_nat[l, 8:30, :], [], [], final=True)
    for g in range(3):
        b.dma(skv_o[g], skv_d[g][NS:, :], [], [], final=True)
    _load_x(b, xT, CTA)

    for l in range(4):
        _mem_kv(b, l, l % 2, memT, w_mem_kv, memkv_o)
        if l == 1:
            b.flush_items()
    late_items = b.items
    b.items = []

    for l in range(2):
        par = l % 2
        w_in = a_w_in[l]
        b.dma(b.ckT[par][:], ckT_d[:, l], [], [("ckT", par)], eng="gpsimd")
        b.dma(b.cv_[par][:], cv_d[:, l], [], [("cvv", par)], eng="gpsimd")
        b.dma(exts[:, :, 0:30], convst_d[:, l], [], [("exts", j) for j in range(NJ)], eng="gpsimd")

        for j in range(NJ):
            def comp(slab, sres, j=j, l=l):
                for ci, (c0, n) in enumerate(CTA):
                    ba, ra = b.bank()
                    bb, rb = b.bank()
                    for kc in range(16):
                        b.mm(ba[:, 0:n], slab[:, kc, 0:128], b.xb[:, kc, c0:c0 + n], kc == 0, kc == 15,
                             [sres, ("xb", kc, ci)], [ra])
                    for kc in range(16):
                        b.mm(bb[:, 0:n], slab[:, kc, 128:256], b.xb[:, kc, c0:c0 + n], kc == 0, kc == 15,
                             [sres, ("xb", kc, ci)], [rb])
                    si, sr = b.scr()
                    sg = b.scr_f(si)[:, 0:n]
                    b.act(sg, bb[:, 0:n], AF.Sigmoid, [rb], [sr])
                    rr = ("ring", j + 1, ci)
                    b.tt(ring[:, j + 1, c0:c0 + n], ba[:, 0:n], sg, ALU.mult, [ra, sr], [rr])
                    if ci == 0:
                        b.ts(ring[:, j + 1, 0:HALO], ring[:, j + 1, 0:HALO], hmask[:, 0:1], None, ALU.mult, None,
                             [rr, "hmask"], [rr])
                    if ci == 2:
                        b.tt(convo[:, l, j, :], ba[:, 34:64], sg[:, 34:64], ALU.mult, [ra, sr], [("convo", l)])
                        b.tt(convsn[:, l, j, :], ba[:, 64:72], sg[:, 64:72], ALU.mult, [ra, sr], [("convsn", l)])
                        b.tt(exts[:, j, 30:38], ba[:, 64:72], sg[:, 64:72], ALU.mult, [ra, sr], [("exts", j)])
            b.add_item(w_in, 16, [(j * 128, 128), (CC + j * 128, 128)], comp)
        b.flush_items()
        b.dma(convp_o[:, l], convo[:, l], [("convo", l)], [], final=True)
        b.dma(convsn_o[:, l], convsn[:, l], [("convsn", l)], [], final=True)

        conv_cts = [(CONV_LO, 512 - CONV_LO), (512, 512), (1024, 64)]
        for j in range(NJ):
            dgs = []
            for k in range(KW):
                di = b.ndg % NDG
                b.ndg += 1
                b.ts(dg[:, di, :], b.ident[:], wdw[:, l, j, k:k + 1], None, ALU.mult, None, ["ident", "wdw"], [("dg", di)])
                dgs.append(di)
            for ci, (c0, n) in enumerate(conv_cts):
                bk, br = b.bank()
                rds = [("ring", j + 1, ci)] + ([("ring", j + 1, ci - 1)] if ci > 0 else [])
                for k in range(KW):
                    b.mm(bk[:, 0:n], dg[:, dgs[k], :], ring[:, j + 1, c0 + k - 30:c0 + k - 30 + n], k == 0, k == KW - 1,
                         [("dg", dgs[k])] + rds, [br])
                b.act(ring[:, j, c0:c0 + n], bk[:, 0:n], AF.Identity, [br, "cnp"], [("ring", j, ci)], bias=cnp[:, l, 0, j:j + 1])
            bk, br = b.bank()
            for k in range(KW):
                b.mm(bk[:, 0:NS], dg[:, dgs[k], :], exts[:, j, k:k + NS], k == 0, k == KW - 1, [("dg", dgs[k]), ("exts", j)], [br])
            b.act(ring[:, j, HALO + CH:NTA], bk[:, 0:NS], AF.Identity, [br, "cnp"], [("ring", j, 2)], bias=cnp[:, l, 0, j:j + 1])

        _stats(b, lambda ci: [(ring[:, j, CTA[ci][0]:CTA[ci][0] + CTA[ci][1]], ("ring", j, ci)) for j in range(NJ)], CTA, CC, LN_EPS)
        for jq in range(NJ // 2):
            def comp(slab, sres, jq=jq, l=l):
                for jo in range(2):
                    j = jq * 2 + jo
                    for ci, (c0, n) in enumerate(CTA):
                        bk, br = b.bank()
                        for kc in range(16):
                            b.mm(bk[:, 0:n], slab[:, kc, jo * 128:(jo + 1) * 128], b.xb[:, kc, c0:c0 + n], kc == 0, kc == 15,
                                 [sres, ("xb", kc, ci)], [br])
                        si, sr = b.scr()
                        sg = b.scr_f(si)[:, 0:n]
                        b.act(sg, bk[:, 0:n], AF.Silu, [br], [sr])
                        si2, sr2 = b.scr()
                        t = b.scr_f(si2)[:, 0:n]
                        rr = ("ring", j, ci)
                        b.tt(t, ring[:, j, c0:c0 + n], b.rstd[:, c0:c0 + n], ALU.mult, [rr, ("rstd", ci)], [sr2])
                        b.tt(t, t, b.shift[:, c0:c0 + n], ALU.add, [sr2, ("shift", ci)], [sr2])
                        b.act(t, t, AF.Silu, [sr2, "cnp"], [sr2], bias=cnp[:, l, 2, j:j + 1], scale=cnp[:, l, 1, j:j + 1])
                        b.stt(ring[:, j, c0:c0 + n], t, INV_ALPHA, sg, ALU.mult, ALU.mult, [sr2, sr], [rr])
            b.add_item(w_in, 16, [(2 * CC + jq * 256, 256)], comp)

        for h in range(4):
            def comp(slab, sres, h=h, par=par):
                for ci, (c0, n) in enumerate(CTA):
                    bq, rq = b.bank()
                    bg, rg = b.bank()
                    for kc in range(16):
                        b.mm(bq[:, 0:n], slab[:, kc, 0:128], b.xb[:, kc, c0:c0 + n], kc == 0, kc == 15, [sres, ("xb", kc, ci)], [rq])
                    for kc in range(16):
                        b.mm(bg[:, 0:n], slab[:, kc, 128:256], b.xb[:, kc, c0:c0 + n], kc == 0, kc == 15, [sres, ("xb", kc, ci)], [rg])
                    si, sr = b.scr()
                    qT = b.scr_b(si)[:, 0:n]
                    b.act(qT, bq[:, 0:n], AF.Copy, [rq], [sr])
                    si2, sr2 = b.scr()
                    sg = b.scr_b(si2)[:, 0:n]
                    b.act(sg, bg[:, 0:n], AF.Silu, [rg], [sr2])
                    b.ts(sg, sg, INV_ALPHA, None, ALU.mult, None, [sr2], [sr2])
                    _mem_attn(b, par, h, qT, sr, sg, sr2, CTA, HALO + CH, ci, c0, n)
            b.add_item(w_in, 16, [(3 * CC + h * 128, 128), (3 * CC + 512 + h * 128, 128)], comp)
        b.flush_items()

        def cm_src(kc, ci, c0, n):
            if kc < NJ:
                return ring[:, kc, c0:c0 + n], ("ring", kc, ci)
            return b.mpart[:, kc - NJ, c0:c0 + n], ("mp", kc - NJ, ci)
        _out_ln(b, l, a_w_out[l], 16, cm_src, CTA)

    for m in range(16):
        for (c0, n, cis) in [(HALO, 512, (0, 1)), (HALO + 512, 512, (1, 2)), (HALO + CH, NS, (2,))]:
            si, sr = b.scr()
            t = b.scr_f(si)[:, 0:n]
            rds = [("xb", m, c) for c in cis] + [("xlo", m, c) for c in cis]
            b.tt(t, b.xb[:, m, c0:c0 + n], b.xlo[:, m, c0:c0 + n], ALU.add, rds, [sr])
            b.dma(x2_o[:, m, c0 - HALO:c0 - HALO + n], t, [sr], [], final=True)

    cts_kv = [(HALO, 512), (HALO + 512, 512), (HALO + CH, NS)]
    cis_of = {0: [0, 1], 1: [1, 2], 2: [2]}
    for fq in range(6):
        def comp(slab, sres, fq=fq):
            for fo in range(2):
                f = fq * 2 + fo
                for ti, (c0, n) in enumerate(cts_kv):
                    bk, br = b.bank()
                    rds = [("xb", kc, c) for kc in range(16) for c in cis_of[ti]]
                    for kc in range(16):
                        b.mm(bk[:, 0:n], slab[:, kc, fo * 128:(fo + 1) * 128], b.xb[:, kc, c0:c0 + n], kc == 0, kc == 15,
                             [sres] + (rds if kc == 0 else []), [br])
                    si, sr = b.scr()
                    t = b.scr_f(si)[:, 0:n]
                    b.act(t, bk[:, 0:n], AF.Copy, [br], [sr])
                    b.dma(kT_o[:, f, c0 - HALO:c0 - HALO + n], t, [sr], [], final=True)
            bk, br = b.bank()
            for kc in range(16):
                b.mm(bk[0:NS, 0:256], b.xb[:, kc, HALO + CH:NTA], slab[:, kc, 0:256], kc == 0, kc == 15,
                     [sres] + ([("xb", k2, 2) for k2 in range(16)] if kc == 0 else []), [br])
            si, sr = b.scr()
            t = b.scr_f(si)[0:NS, 0:256]
            b.act(t, bk[0:NS, 0:256], AF.Copy, [br], [sr])
            b.dma(ks_o[:, fq * 256:(fq + 1) * 256], t, [sr], [], final=True)
        b.add_item(w_kv, 16, [(fq * 256, 256)], comp)
    for vq in range(6):
        def comp(slab, sres, vq=vq):
            tiles = [(HALO + 128 * t, 128) for t in range(8)] + [(HALO + CH, NS)]
            for (c0, n) in tiles:
                bk, br = b.bank()
                ci = min((c0) // 512, 2)
                for kc in range(16):
                    b.mm(bk[0:n, 0:256], b.xb[:, kc, c0:c0 + n], slab[:, kc, 0:256], kc == 0, kc == 15,
                         [sres] + ([("xb", k2, c) for k2 in range(16) for c in (ci, min(ci + 1, 2))] if kc == 0 else []), [br])
                si, sr = b.scr()
                t = b.scr_f(si)[0:n, 0:256]
                b.act(t, bk[0:n, 0:256], AF.Copy, [br], [sr])
                b.dma(v_o[c0 - HALO:c0 - HALO + n, vq * 256:(vq + 1) * 256], t, [sr], [], final=True)
        b.add_item(w_kv, 16, [(CC + vq * 256, 256)], comp)
    b.items = b.items + late_items
    b.flush_items()
    b.p.emit(nc, b.st)
    return b


CTB = [(0, 512), (512, 512), (1024, NS)]
LQ = [CH // d for (_, d) in GROUPS]
NK = [128 + q for q in LQ]
NCH = [(k + 127) // 128 for k in NK]
NST = [w // 128 for (w, _) in GROUPS]
LS = [w + 16 for (w, _) in GROUPS]
OGS = [128, 128 + LS[0], 128 + LS[0] + LS[1]]
TABS = 128 + sum(LS)
NEG = -30000.0


def build_B():
    b = Builder("B")
    nc = b.nc
    x2T = b.din("x2T", [128, 16, NTB])
    ident_d = b.din("identf", [128, 128])
    b_w_in = b.din("b_w_in", [2, D, B_IN])
    b_w_out = b.din("b_w_out", [2, 1024, D])
    lnp_d = b.din("lnp", [128, 4, 2, 16])
    mkT_d = b.din("mkT", [128, 2, 4, NMEM])
    mv_d = b.din("mv", [128, 2, 2, 512])
    ckT_d = b.din("ckT", [128, 4, 4, NMEM])
    cv_d = b.din("cv", [128, 4, 2, 512])
    relb_d = b.din("relb", [32, 12])
    ohp_d = b.din("ohp", [3, 33, 384])
    ohs_d = b.din("ohs", [33, sum(LS)])
    KT_d = [b.din("KT%d" % g, [128, 4, GROUPS[g][1], NK[g]]) for g in range(3)]
    V_d = [b.din("V%d" % g, [128, GROUPS[g][1], NCH[g], 512]) for g in range(3)]
    kval_d = [b.din("kval%d" % g, [128, GROUPS[g][1] * NCH[g]]) for g in range(3)]
    sKT_d = [b.din("sKT%d" % g, [128, 4, GROUPS[g][0]]) for g in range(3)]
    sV_d = [b.din("sV%d" % g, [128, NST[g], 512]) for g in range(3)]
    sKTn_d = b.din("sKTn", [128, 12, NS])
    sVn_d = b.din("sVn", [NS, CC])
    y_o = b.dout("y_o", [128, 16, NTB])
    tabp_h = nc.dram_tensor("tabp_scr", [12, 384], F32)
    tabs_h = nc.dram_tensor("tabs_scr", [4, TABS], F32)

    _common_decl(b, NTB, 2, 1, False)
    om = b.sb("om", [128, 8, NTB], BF16)
    b.mpart = om[:, 4:8, :]
    relb = b.sb("relb", [33, 12], F32)
    E = b.sb("E", [128, 12, 256], BF16)
    E2 = b.sb("E2", [64, 4, 256], BF16)
    Es = [b.sb("Es%d" % g, [128, 4, NST[g] + 1, NS], F32) for g in range(3)]
    kval = [b.sb("kval%d" % g, [128, GROUPS[g][1] * NCH[g]], F32) for g in range(3)]
    qT = [b.sb("qT%d" % g, [128, CH], BF16) for g in range(3)]
    qTs = b.sb("qTs", [128, 3, NS], BF16)
    sgate = b.sb("sgate", [128, NTB], BF16)
    kt = [b.sb("kt%d" % g, [128, GROUPS[g][1], NK[g]], BF16) for g in range(3)]
    vt = [b.sb("vt%d" % g, [128, GROUPS[g][1], NCH[g], 128], BF16) for g in range(3)]
    skt = [b.sb("skt%d" % g, [128, GROUPS[g][0] + NS], BF16) for g in range(3)]
    sv = [b.sb("sv%d" % g, [128, NST[g] + 1, 128], BF16) for g in range(3)]
    NUM = b.sb("NUM", [128, CH], F32)
    DEN = b.sb("DEN", [128, CH], F32)

    b.dma(b.ident[:], ident_d, [], ["ident"], eng="gpsimd")
    b.p.op("vector", lambda e: e.memset(b.ones[:], 1.0), [], ["ones"])
    b.p.op("vector", lambda e: e.memset(om[:], 0.0), [], [("om", k, ci) for k in range(4) for ci in range(3)] +
           [("mp", k, ci) for k in range(4) for ci in range(3)])
    b.dma(b.lnp[:], lnp_d, [], ["lnp"])
    b.p.op("vector", lambda e: e.memset(relb[32:33, :], NEG), [], ["relb"])
    b.dma(relb[0:32, :], relb_d, ["relb"], ["relb"])
    for g in range(3):
        b.dma(kval[g][:], kval_d[g], [], [("kval", g)])
    _load_x(b, x2T, CTB)

    for g in range(3):
        si, sr = b.scr()
        oh = b.scr_f(si)[0:33, 0:384]
        b.dma(oh, ohp_d[g], [], [sr])
        bk, br = b.bank()
        b.mm(bk[0:4, 0:384], relb[0:33, g * 4:(g + 1) * 4], oh, True, True, ["relb", sr], [br])
        si2, sr2 = b.scr()
        t = b.scr_f(si2)[0:4, 0:384]
        b.cp(t, bk[0:4, 0:384], [br], [sr2])
        b.dma(tabp_h.ap()[g * 4:(g + 1) * 4, :], t, [sr2], ["tabp"])
    for gh in range(12):
        si, sr = b.scr()
        t = b.scr_f(si)[:, 0:256]
        src = bass.AP(tensor=tabp_h, offset=gh * 384, ap=[[1, 128], [1, 256]])
        b.dma(t, src, ["tabp"], [sr])
        b.act(E[:, gh, :], t, AF.Exp, [sr], [("E", gh)])
    for h in range(4):
        si, sr = b.scr()
        t = b.scr_f(si)[0:64, 0:256]
        src = bass.AP(tensor=tabp_h, offset=(8 + h) * 384 + 64, ap=[[1, 64], [1, 256]])
        b.dma(t, src, ["tabp"], [sr])
        b.act(E2[0:64, h, :], t, AF.Exp, [sr], [("E2", h)])
    off = 0
    for g in range(3):
        for pc in range(0, LS[g], 512):
            n = min(512, LS[g] - pc)
            si, sr = b.scr()
            oh = b.scr_f(si)[0:33, 0:n]
            b.dma(oh, ohs_d[:, off + pc:off + pc + n], [], [sr])
            bk, br = b.bank()
            b.mm(bk[0:4, 0:n], relb[0:33, g * 4:(g + 1) * 4], oh, True, True, ["relb", sr], [br])
            si2, sr2 = b.scr()
            t = b.scr_f(si2)[0:4, 0:n]
            b.cp(t, bk[0:4, 0:n], [br], [sr2])
            b.dma(tabs_h.ap()[:, OGS[g] + pc:OGS[g] + pc + n], t, [sr2], ["tabs"])
        off += LS[g]
    for g in range(3):
        W = GROUPS[g][0]
        for h in range(4):
            src = bass.AP(tensor=tabs_h, offset=h * TABS + OGS[g] + 9, ap=[[1, 128], [128, NST[g]], [1, NS]])
            b.dma(Es[g][:, h, 0:NST[g], :], src, ["tabs"], [("Es", g, h)])
            src2 = bass.AP(tensor=tabs_h, offset=h * TABS + OGS[g] + 1, ap=[[1, NS], [1, NS]])
            b.dma(Es[g][0:NS, h, NST[g], :], src2, ["tabs"], [("Es", g, h)])
            b.act(Es[g][:, h, 0:NST[g], :], Es[g][:, h, 0:NST[g], :], AF.Exp, [("Es", g, h)], [("Es", g, h)])
            b.act(Es[g][0:NS, h, NST[g], :], Es[g][0:NS, h, NST[g], :], AF.Exp, [("Es", g, h)], [("Es", g, h)])

    def load_kv(h):
        for g in range(3):
            d = GROUPS[g][1]
            W = GROUPS[g][0]
            b.dma(kt[g][:], KT_d[g][:, h], [], [("kt", g)], eng="gpsimd")
            b.dma(vt[g][:], V_d[g][:, :, :, h * 128:(h + 1) * 128], [], [("vt", g)], eng="gpsimd")
            b.dma(skt[g][:, 0:W], sKT_d[g][:, h, :], [], [("skt", g)], eng="gpsimd")
            b.dma(skt[g][:, W:W + NS], sKTn_d[:, g * 4 + h, :], [], [("skt", g)], eng="gpsimd")
            b.dma(sv[g][:, 0:NST[g], :], sV_d[g][:, :, h * 128:(h + 1) * 128], [], [("sv", g)], eng="gpsimd")
            b.dma(sv[g][0:NS, NST[g], :], sVn_d[:, (g * 4 + h) * 128:(g * 4 + h + 1) * 128], [], [("sv", g)], eng="gpsimd")

    def dil_attn(h):
        for g in range(3):
            d = GROUPS[g][1]
            lq, nch = LQ[g], NCH[g]
            gh = g * 4 + h
            for c in range(d):
                pts = {}
                for i in range(nch):
                    klo = -128 + 128 * i
                    ks = min(128, NK[g] - 128 * i)
                    qs = max(0, klo)
                    qe = min(lq, klo + ks + 128)
                    nq = qe - qs
                    if nq > 0:
                        bk, br = b.bank()
                        b.mm(bk[0:ks, 0:nq], kt[g][:, c, 128 * i:128 * i + ks], qT[g][:, c * lq + qs:c * lq + qe], True, True,
                             [("kt", g), ("qT", g)], [br])
                        si, sr = b.scr()
                        tmp = b.scr_f(si)[0:ks, 0:nq]
                        b.act(tmp, bk[0:ks, 0:nq], AF.Exp, [br], [sr], scale=SCALE)
                        si2, sr2 = b.scr()
                        pt = b.scr_b(si2)[0:ks, 0:nq]
                        if ks == 128:
                            eap, eres = E[:, gh, qs - klo:qs - klo + nq], ("E", gh)
                        else:
                            assert ks == 64 and g == 2
                            eap, eres = E2[0:64, h, qs - klo:qs - klo + nq], ("E2", h)
                        b.stt(pt, tmp, kval[g][0:ks, c * nch + i:c * nch + i + 1], eap,
                              ALU.mult, ALU.mult, [sr, ("kval", g), eres], [sr2])
                        pts[i] = (b.scr_b(si2), sr2, qs, ks)
                    qb = i - 1
                    if qb < 0 or 128 * qb >= lq:
                        continue
                    q0 = 128 * qb
                    nqb = min(128, lq - q0)
                    contrib = [ii for ii in (qb, qb + 1) if ii in pts]
                    nb_, nr_ = b.bank()
                    db_, dr_ = b.bank()
                    for k, ii in enumerate(contrib):
                        ptf, prs, pqs, pks = pts[ii]
                        b.mm(nb_[:, 0:nqb], vt[g][0:pks, c, ii, :], ptf[0:pks, q0 - pqs:q0 - pqs + nqb], k == 0,
                             k == len(contrib) - 1, [("vt", g), prs], [nr_])
                    for k, ii in enumerate(contrib):
                        ptf, prs, pqs, pks = pts[ii]
                        b.mm(db_[:, 0:nqb], b.ones[0:pks, :], ptf[0:pks, q0 - pqs:q0 - pqs + nqb], k == 0,
                             k == len(contrib) - 1, ["ones", prs], [dr_])
                    t0 = d * q0 + c
                    nsl = NUM[:, t0:t0 + d * (nqb - 1) + 1:d]
                    dsl = DEN[:, t0:t0 + d * (nqb - 1) + 1:d]
                    if g == 0:
                        b.act(nsl, nb_[:, 0:nqb], AF.Copy, [nr_], ["NUM"])
                        b.cp(dsl, db_[:, 0:nqb], [dr_], ["DEN"])
                    else:
                        b.tt(nsl, nb_[:, 0:nqb], nsl, ALU.add, [nr_, "NUM"], ["NUM"])
                        b.tt(dsl, db_[:, 0:nqb], dsl, ALU.add, [dr_, "DEN"], ["DEN"])
        for ci, (c0, n) in enumerate(CTB[0:2]):
            si, sr = b.scr()
            rd = b.scr_f(si)[:, 0:n]
            b.p.op("vector", lambda e, rd=rd, c0=c0, n=n: e.reciprocal(rd, DEN[:, c0:c0 + n]), ["DEN"], [sr])
            b.tt(rd, NUM[:, c0:c0 + n], rd, ALU.mult, ["NUM", sr], [sr])
            b.tt(om[:, h, c0:c0 + n], rd, sgate[:, c0:c0 + n], ALU.mult, [sr, "sgate"], [("om", h, ci)])
        nb_, nr_ = b.bank()
        db_, dr_ = b.bank()
        first = True
        for g in range(3):
            W = GROUPS[g][0]
            nst = NST[g]
            pieces = []
            bk, br = b.bank()
            for ch in range(nst):
                b.mm(bk[:, ch * NS:(ch + 1) * NS], skt[g][:, ch * 128:(ch + 1) * 128], qTs[:, g, :], True, True,
                     [("skt", g), "qTs"], [br])
            si, sr = b.scr()
            tmp = b.scr_f(si)[:, 0:nst * NS]
            b.act(tmp, bk[:, 0:nst * NS], AF.Exp, [br], [sr], scale=SCALE)
            si2, sr2 = b.scr()
            pt = b.scr_b(si2)[:, 0:nst * NS]
            b.tt(pt.rearrange("p (c q) -> p c q", q=NS), tmp.rearrange("p (c q) -> p c q", q=NS), Es[g][:, h, 0:nst, :],
                 ALU.mult, [sr, ("Es", g, h)], [sr2])
            for ch in range(nst):
                pieces.append((pt[:, ch * NS:(ch + 1) * NS], sr2, 128, ch))
            bk, br = b.bank()
            b.mm(bk[0:NS, 0:NS], skt[g][:, W:W + NS], qTs[:, g, :], True, True, [("skt", g), "qTs"], [br])
            si, sr = b.scr()
            tmp = b.scr_f(si)[0:NS, 0:NS]
            b.act(tmp, bk[0:NS, 0:NS], AF.Exp, [br], [sr], scale=SCALE)
            si3, sr3 = b.scr()
            pt2 = b.scr_b(si3)[0:NS, 0:NS]
            b.tt(pt2, tmp, Es[g][0:NS, h, nst, :], ALU.mult, [sr, ("Es", g, h)], [sr3])
            pieces.append((pt2, sr3, NS, nst))
            for k, (pp, prs, ks, ch) in enumerate(pieces):
                last = (g == 2 and k == len(pieces) - 1)
                b.mm(nb_[:, 0:NS], sv[g][0:ks, ch, :], pp, first, last, [("sv", g), prs], [nr_])
                b.mm(db_[:, 0:NS], b.ones[0:ks, :], pp, first, last, ["ones", prs], [dr_])
                first = False
        si, sr = b.scr()
        rd = b.scr_f(si)[:, 0:NS]
        b.p.op("vector", lambda e, rd=rd, db_=db_: e.reciprocal(rd, db_[:, 0:NS]), [dr_], [sr])
        b.tt(rd, nb_[:, 0:NS], rd, ALU.mult, [nr_, sr], [sr])
        b.tt(om[:, h, CH:NTB], rd, sgate[:, CH:NTB], ALU.mult, [sr, "sgate"], [("om", h, 2)])

    for l in (2, 3):
        i_ = l - 2
        w_in = b_w_in[i_]
        b.dma(b.mkT[0][:], mkT_d[:, i_], [], [("mkT", 0)], eng="gpsimd")
        b.dma(b.mv[0][:], mv_d[:, i_], [], [("mv", 0)], eng="gpsimd")
        b.dma(b.ckT[0][:], ckT_d[:, l], [], [("ckT", 0)], eng="gpsimd")
        b.dma(b.cv_[0][:], cv_d[:, l], [], [("cvv", 0)], eng="gpsimd")
        for h in range(4):
            def comp1(slab, sres, h=h):
                load_kv(h)
                for g in (0, 1):
                    d = GROUPS[g][1]
                    for ci, (c0, n) in enumerate(CTB):
                        bk, br = b.bank()
                        for kc in range(16):
                            b.mm(bk[:, 0:n], slab[:, kc, g * 128:(g + 1) * 128], b.xb[:, kc, c0:c0 + n], kc == 0, kc == 15,
                                 [sres, ("xb", kc, ci)], [br])
                        if ci < 2:
                            dst = qT[g][:, :].rearrange("p (c m) -> p c m", c=d)[:, :, c0 // d:(c0 + n) // d]
                            src = bk[:, 0:n].rearrange("p (m c) -> p c m", c=d)
                            b.act(dst, src, AF.Copy, [br], [("qT", g)])
                        else:
                            b.act(qTs[:, g, :], bk[:, 0:n], AF.Copy, [br], ["qTs"])

            def comp2(slab, sres, h=h):
                g = 2
                d = GROUPS[g][1]
                for ci, (c0, n) in enumerate(CTB):
                    bk, br = b.bank()
                    for kc in range(16):
                        b.mm(bk[:, 0:n], slab[:, kc, 0:128], b.xb[:, kc, c0:c0 + n], kc == 0, kc == 15,
                             [sres, ("xb", kc, ci)], [br])
                    if ci < 2:
                        dst = qT[g][:, :].rearrange("p (c m) -> p c m", c=d)[:, :, c0 // d:(c0 + n) // d]
                        src = bk[:, 0:n].rearrange("p (m c) -> p c m", c=d)
                        b.act(dst, src, AF.Copy, [br], [("qT", g)])
                    else:
                        b.act(qTs[:, g, :], bk[:, 0:n], AF.Copy, [br], ["qTs"])
                    bg, rg = b.bank()
                    for kc in range(16):
                        b.mm(bg[:, 0:n], slab[:, kc, 128:256], b.xb[:, kc, c0:c0 + n], kc == 0, kc == 15,
                             [sres, ("xb", kc, ci)], [rg])
                    b.act(sgate[:, c0:c0 + n], bg[:, 0:n], AF.Silu, [rg], ["sgate"])
                    b.ts(sgate[:, c0:c0 + n], sgate[:, c0:c0 + n], INV_ALPHA, None, ALU.mult, None, ["sgate"], ["sgate"])
                dil_attn(h)

            def comp3(slab, sres, h=h):
                for ci, (c0, n) in enumerate(CTB):
                    bq, rq = b.bank()
                    bg, rg = b.bank()
                    for kc in range(16):
                        b.mm(bq[:, 0:n], slab[:, kc, 0:128], b.xb[:, kc, c0:c0 + n], kc == 0, kc == 15, [sres, ("xb", kc, ci)], [rq])
                    for kc in range(16):
                        b.mm(bg[:, 0:n], slab[:, kc, 128:256], b.xb[:, kc, c0:c0 + n], kc == 0, kc == 15, [sres, ("xb", kc, ci)], [rg])
                    si, sr = b.scr()
                    qm = b.scr_b(si)[:, 0:n]
                    b.act(qm, bq[:, 0:n], AF.Copy, [rq], [sr])
                    si2, sr2 = b.scr()
                    sg = b.scr_b(si2)[:, 0:n]
                    b.act(sg, bg[:, 0:n], AF.Silu, [rg], [sr2])
                    b.ts(sg, sg, INV_ALPHA, None, ALU.mult, None, [sr2], [sr2])
                    _mem_attn(b, 0, h, qm, sr, sg, sr2, CTB, CH, ci, c0, n)
            b.add_item(w_in, 16, [(0 * 512 + h * 128, 128), (1 * 512 + h * 128, 128)], comp1)
            b.add_item(w_in, 16, [(2 * 512 + h * 128, 128), (CC + h * 128, 128)], comp2)
            b.add_item(w_in, 16, [(2048 + h * 128, 128), (2560 + h * 128, 128)], comp3)
        b.flush_items()

        def cm_src(kc, ci, c0, n):
            if kc < 4:
                return om[:, kc, c0:c0 + n], ("om", kc, ci)
            return om[:, kc, c0:c0 + n], ("mp", kc - 4, ci)
        _out_ln(b, l, b_w_out[i_], 8, cm_src, CTB, last_out=(y_o if l == 3 else None))
    b.p.emit(nc, b.st)
    return b


def _fm(a):
    t, f = a.shape
    return np.ascontiguousarray(a.reshape(t, f // 128, 128).transpose(2, 1, 0))


def _unfm(a):
    p, c, t = a.shape
    return np.ascontiguousarray(a.transpose(2, 1, 0).reshape(t, c * 128))


_CACHE = {}


def run_A(inp):
    if "A" not in _CACHE:
        _CACHE["A"] = build_A()
    b = _CACHE["A"]
    xp = inp["x_prompt"]
    xs = inp["x_sample"]
    in_maps = []
    ident = np.eye(128, dtype=np.float32)
    wdw = np.ascontiguousarray(inp["a_w_dw"].reshape(2, KW, NJ, 128).transpose(3, 0, 2, 1))
    cnp = np.stack([inp["a_b_dw"], inp["a_cn_g"], inp["a_cn_b"]], axis=1)
    cnp = np.ascontiguousarray(cnp.reshape(2, 3, NJ, 128).transpose(3, 0, 1, 2))
    lnp = np.stack([inp["ln_g"], inp["ln_b"]], axis=1)
    lnp = np.ascontiguousarray(lnp.reshape(4, 2, 16, 128).transpose(3, 0, 1, 2))
    for r in range(NCORE):
        bp, c = r // 4, r % 4
        t0 = c * CH
        cols = np.zeros((NTA, D), np.float32)
        if c > 0:
            cols[0:HALO] = xp[bp, t0 - HALO:t0]
        cols[HALO:HALO + CH] = xp[bp, t0:t0 + CH]
        cols[HALO + CH:] = xs[r]
        st = inp["state_conv"][:, r]
        convst = np.ascontiguousarray(st.reshape(2, 30, NJ, 128).transpose(3, 0, 2, 1))
        cm = inp["cache_mem_kv"][:, r]
        ckT = np.ascontiguousarray(cm[:, :, 0].transpose(3, 0, 2, 1))
        cv = np.ascontiguousarray(cm[:, :, 1].reshape(4, 2, 128, 512).transpose(2, 0, 1, 3))
        in_maps.append({
            "xT": _fm(cols), "memT": _fm(inp["mem_prompt"][bp]), "identf": ident,
            "hmask": np.full((128, 1), 0.0 if c == 0 else 1.0, np.float32),
            "a_w_in": inp["a_w_in"], "a_w_out": inp["a_w_out"], "w_kv": inp["w_kv_shared"],
            "w_mem_kv": inp["w_mem_kv"], "wdw": wdw, "cnp": cnp, "lnp": lnp,
            "convst": convst, "convst_nat": np.ascontiguousarray(st), "ckT": ckT, "cv": cv,
            "skv0": np.ascontiguousarray(inp["state_kv_g0"][r].reshape(-1, 1024)),
            "skv1": np.ascontiguousarray(inp["state_kv_g1"][r].reshape(-1, 1024)),
            "skv2": np.ascontiguousarray(inp["state_kv_g2"][r].reshape(-1, 1024)),
        })
    res = run_bass_kernel_spmd(b.nc, in_maps, core_ids=list(range(NCORE)))
    return res.results


def _bucket_np(dist):
    import math
    max_exact = 16
    safe = np.maximum(dist, 1).astype(np.float32)
    large = max_exact + (np.log(safe / np.float32(max_exact)) / np.float32(math.log(2048 / max_exact))
                         * np.float32(32 - max_exact)).astype(np.int32)
    large = np.minimum(large, 31)
    return np.where(dist < max_exact, dist, large)


def _onehots():
    ohp = np.zeros((3, 33, 384), np.float32)
    ohs = []
    for g, (w, d) in enumerate(GROUPS):
        for n in range(384):
            j = n - 127
            if 0 <= j <= 128:
                ohp[g, int(_bucket_np(np.array([d * j]))[0]), n] = 1.0
            else:
                ohp[g, 32, n] = 1.0
        o = np.zeros((33, LS[g]), np.float32)
        for n in range(LS[g]):
            dist = n - 8
            if dist >= 0 and dist % d == 0 and dist // d <= w // d:
                o[int(_bucket_np(np.array([dist]))[0]), n] = 1.0
            else:
                o[32, n] = 1.0
        ohs.append(o)
    return ohp, np.concatenate(ohs, axis=1)


def run_B(inp, ra):
    if "B" not in _CACHE:
        _CACHE["B"] = build_B()
    b = _CACHE["B"]
    ident = np.eye(128, dtype=np.float32)
    lnp = np.stack([inp["ln_g"], inp["ln_b"]], axis=1)
    lnp = np.ascontiguousarray(lnp.reshape(4, 2, 16, 128).transpose(3, 0, 1, 2))
    ohp, ohs = _onehots()
    Kfull, Vfull = [], []
    for bp in range(2):
        Kfull.append(np.concatenate([_unfm(ra[bp * 4 + c]["kT_o"][:, :, :CH]) for c in range(4)], axis=0))
        Vfull.append(np.concatenate([ra[bp * 4 + c]["v_o"][:CH] for c in range(4)], axis=0))
    in_maps = []
    for r in range(NCORE):
        bp, c = r // 4, r % 4
        t0 = c * CH
        mk = ra[bp * 4]["memkv_o"]
        mkT = np.ascontiguousarray(mk[2:4, :, 0:512].reshape(2, NMEM, 4, 128).transpose(3, 0, 2, 1))
        mv = np.ascontiguousarray(mk[2:4, :, 512:1024].reshape(2, 2, 128, 512).transpose(2, 0, 1, 3))
        cm = inp["cache_mem_kv"][:, r]
        ckT = np.ascontiguousarray(cm[:, :, 0].transpose(3, 0, 2, 1))
        cv = np.ascontiguousarray(cm[:, :, 1].reshape(4, 2, 128, 512).transpose(2, 0, 1, 3))
        m = {"x2T": ra[r]["x2_o"], "identf": ident, "b_w_in": inp["b_w_in"], "b_w_out": inp["b_w_out"], "lnp": lnp,
             "mkT": mkT, "mv": mv, "ckT": ckT, "cv": cv, "relb": inp["rel_bias"], "ohp": ohp, "ohs": ohs,
             "sKTn": np.ascontiguousarray(ra[r]["kT_o"][:, :, CH:NTB][:, :, ::-1]),
             "sVn": np.ascontiguousarray(ra[r]["v_o"][CH:NTB][::-1])}
        for g, (w, d) in enumerate(GROUPS):
            nk, nch = NK[g], NCH[g]
            tok = t0 + d * (np.arange(nk)[None, :] - 128) + np.arange(d)[:, None]
            valid = tok >= 0
            tk = np.maximum(tok, 0)
            Kg = Kfull[bp][:, g * 512:(g + 1) * 512].reshape(SEQ, 4, 128)
            Vg = Vfull[bp][:, g * 512:(g + 1) * 512]
            perm = np.concatenate([np.arange(min(128, nk - 128 * ch))[::-1] + 128 * ch for ch in range(nch)])
            tk = tk[:, perm]
            valid = valid[:, perm]
            kt = Kg[tk] * valid[:, :, None, None]
            m["KT%d" % g] = np.ascontiguousarray(kt.transpose(3, 2, 0, 1)).astype(np.float32)
            vpad = np.zeros((d, nch * 128, 512), np.float32)
            vpad[:, :nk] = Vg[tk] * valid[:, :, None]
            m["V%d" % g] = np.ascontiguousarray(vpad.reshape(d, nch, 128, 512).transpose(2, 0, 1, 3))
            kv_ = np.zeros((d, nch * 128), np.float32)
            kv_[:, :nk] = valid
            m["kval%d" % g] = np.ascontiguousarray(kv_.reshape(d, nch, 128).transpose(2, 0, 1).reshape(128, d * nch))
            stt = inp["state_kv_g%d" % g][r][::-1]
            m["sKT%d" % g] = np.ascontiguousarray(stt[:, 0].transpose(2, 1, 0))
            m["sV%d" % g] = np.ascontiguousarray(stt[:, 1].reshape(w // 128, 128, 512).transpose(1, 0, 2))
        in_maps.append(m)
    res = run_bass_kernel_spmd(b.nc, in_maps, core_ids=list(range(NCORE)))
    return res.results


def assemble(inp, ra, rb):
    y_p = np.zeros((2, SEQ, D), np.float32)
    y_s = np.zeros((NCORE, NS, D), np.float32)
    for r in range(NCORE):
        bp, c = r // 4, r % 4
        yo = rb[r]["y_o"]
        y_p[bp, c * CH:(c + 1) * CH] = _unfm(yo[:, :, :CH])
        y_s[r] = _unfm(yo[:, :, CH:])
    conv_p = np.stack([np.stack([_unfm(ra[r]["convp_o"][:, l]) for r in (3, 7)]) for l in range(2)])
    conv_s = np.zeros((2, NCORE, 30, CC), np.float32)
    for r in range(NCORE):
        for l in range(2):
            conv_s[l, r, :22] = ra[r]["convss_o"][l]
            conv_s[l, r, 22:] = _unfm(ra[r]["convsn_o"][:, l])
    outs = [y_p, y_s, conv_p, conv_s]
    for g, (w, d) in enumerate(GROUPS):
        rows = min(w, SEQ)
        kp = np.zeros((2, rows, 2, 4, 128), np.float32)
        for bp in range(2):
            kk = np.concatenate([_unfm(ra[bp * 4 + c]["kT_o"][:, g * 4:(g + 1) * 4, :CH]) for c in range(4)], axis=0)
            vv = np.concatenate([ra[bp * 4 + c]["v_o"][:CH, g * 512:(g + 1) * 512] for c in range(4)], axis=0)
            kp[bp, :, 0] = kk[-rows:].reshape(rows, 4, 128)
            kp[bp, :, 1] = vv[-rows:].reshape(rows, 4, 128)
        ks = np.zeros((NCORE, w, 2, 4, 128), np.float32)
        for r in range(NCORE):
            ks[r, :w - NS] = ra[r]["skv_o%d" % g].reshape(w - NS, 2, 4, 128)
            ks[r, w - NS:, 0] = ra[r]["ks_o"][:, g * 512:(g + 1) * 512].reshape(NS, 4, 128)
            ks[r, w - NS:, 1] = ra[r]["v_o"][CH:NTB, g * 512:(g + 1) * 512].reshape(NS, 4, 128)
        outs += [kp, ks]
    mk = np.stack([ra[0]["memkv_o"], ra[4]["memkv_o"]], axis=1).reshape(4, 2, NMEM, 2, 4, 128)
    outs.append(np.ascontiguousarray(mk))
    return tuple(outs)


def kernel(**inp):
    inp = {k: np.asarray(v) for k, v in inp.items()}
    ra = run_A(inp)
    rb = run_B(inp, ra)
    return assemble(inp, ra, rb)
```

```python
import numpy as np
from contextlib import ExitStack
import concourse.bass as bass
import concourse.mybir as mybir
from concourse.bass_utils import run_bass_kernel_spmd

F32 = mybir.dt.float32
BF16 = mybir.dt.bfloat16
I32 = mybir.dt.int32
AF = mybir.ActivationFunctionType
ALU = mybir.AluOpType

D = 2048
DEPTH = 4
SEQ = 4096
NCORE = 8
CH = 1024
HALO = 64
NS = 8
NTA = HALO + CH + NS
NTB = CH + NS
CC = 1536
NJ = 12
KW = 31
A_IN = 5632
B_IN = 3072
NMEM = 256
LN_EPS = 1e-5
ALPHA = (2 * DEPTH) ** 0.25
INV_ALPHA = 1.0 / ALPHA
EPS_Z = LN_EPS / (ALPHA * ALPHA)
SCALE = 128 ** -0.5
GROUPS = ((128, 1), (512, 4), (2048, 16))

ENGS = ("tensor", "vector", "scalar", "gpsimd", "sync")
N_DMA_SEMS = 12


class Op:
    __slots__ = ("eng", "fn", "dma", "deps", "signal", "sigval", "sem", "idx")

    def __init__(self, eng, fn, dma):
        self.eng = eng
        self.fn = fn
        self.dma = dma
        self.deps = []
        self.signal = False
        self.sigval = 0
        self.sem = None
        self.idx = 0


class _St:
    __slots__ = ("w", "r")

    def __init__(self):
        self.w = None
        self.r = {}


class Prog:
    def __init__(self):
        self.ops = {e: [] for e in ENGS}
        self.res = {}
        self.final_dmas = []

    def op(self, eng, fn, reads=(), writes=(), dma=False, final=False):
        o = Op(eng, fn, dma)
        o.idx = len(self.ops[eng])
        self.ops[eng].append(o)
        deps = {}
        for r in reads:
            st = self.res.get(r)
            if st is not None and st.w is not None:
                deps[id(st.w)] = st.w
        for w in writes:
            st = self.res.get(w)
            if st is not None:
                if st.w is not None:
                    deps[id(st.w)] = st.w
                for q in st.r.values():
                    deps[id(q)] = q
        for q in deps.values():
            if q is o:
                continue
            if (not q.dma) and (not dma) and q.eng == eng and eng == "tensor":
                continue
            q.signal = True
            o.deps.append(q)
        for r in reads:
            st = self.res.setdefault(r, _St())
            st.r[("dma", id(o)) if dma else eng] = o
        for w in writes:
            st = self.res.setdefault(w, _St())
            st.w = o
            st.r = {}
        if final:
            o.signal = True
            self.final_dmas.append(o)
        return o

    def emit(self, nc, stack):
        sems = {}
        for e in ("tensor", "vector", "scalar", "gpsimd"):
            sems[e] = stack.enter_context(nc.semaphore("c_" + e))
        dsems = {}
        for e in ("sync", "gpsimd", "scalar"):
            dsems[e] = [stack.enter_context(nc.semaphore("d_%s%d" % (e, i))) for i in range(N_DMA_SEMS)]
        for e in ENGS:
            cnt = 0
            dcnt = 0
            dvals = [0] * N_DMA_SEMS
            for o in self.ops[e]:
                if o.dma:
                    k = dcnt % N_DMA_SEMS
                    dcnt += 1
                    dvals[k] += 16
                    o.sem = dsems[e][k]
                    o.sigval = dvals[k]
                elif o.signal:
                    cnt += 1
                    o.sem = sems[e]
                    o.sigval = cnt
        block = stack.enter_context(nc.Block())
        prog = self

        def run(e):
            def body(eng):
                waited = {}
                last_on_sem = {}
                for o in prog.ops[e]:
                    if o.dma:
                        prev = last_on_sem.get(id(o.sem))
                        if prev is not None and waited.get(id(o.sem), 0) < prev:
                            eng.wait_ge(o.sem, prev)
                            waited[id(o.sem)] = prev
                    for q in o.deps:
                        k = id(q.sem)
                        if waited.get(k, 0) < q.sigval:
                            eng.wait_ge(q.sem, q.sigval)
                            waited[k] = q.sigval
                    ins = o.fn(eng)
                    if o.dma:
                        ins.then_inc(o.sem, 16)
                        last_on_sem[id(o.sem)] = o.sigval
                    elif o.signal:
                        ins.then_inc(o.sem, 1)
                if e == "sync":
                    for o in prog.final_dmas:
                        k = id(o.sem)
                        if waited.get(k, 0) < o.sigval:
                            eng.wait_ge(o.sem, o.sigval)
                            waited[k] = o.sigval
            return body

        for e in ENGS:
            if self.ops[e] or e == "sync":
                getattr(block, e)(run(e))


class Builder:
    def __init__(self, phase):
        self.phase = phase
        self.nc = bass.Bass("TRN2", target_bir_lowering=False)
        self.p = Prog()
        self.st = ExitStack()
        self.nbank = 0
        self.nscr = 0
        self.ndg = 0
        self.items = []

    def din(self, name, shape, dt=F32):
        return self.nc.dram_tensor(name, list(shape), dt, kind="ExternalInput").ap()

    def dout(self, name, shape, dt=F32):
        return self.nc.dram_tensor(name, list(shape), dt, kind="ExternalOutput").ap()

    def sb(self, name, shape, dt):
        return self.st.enter_context(self.nc.sbuf_tensor("s_" + name, list(shape), dt))

    def mm(self, out, lhsT, rhs, start, stop, reads, writes):
        self.p.op("tensor", lambda e: e.matmul(out, lhsT, rhs, start=start, stop=stop), reads, writes)

    def act(self, out, in_, func, reads, writes, bias=None, scale=None):
        kw = {}
        if bias is not None:
            kw["bias"] = bias
        if scale is not None:
            kw["scale"] = scale
        self.p.op("scalar", lambda e: e.activation(out=out, in_=in_, func=func, **kw), reads, writes)

    def tt(self, out, a, b, op, reads, writes, eng="vector"):
        self.p.op(eng, lambda e: e.tensor_tensor(out, a, b, op), reads, writes)

    def ts(self, out, a, s1, s2, op0, op1, reads, writes, eng="vector"):
        if s2 is None:
            self.p.op(eng, lambda e: e.tensor_scalar(out, a, s1, None, op0), reads, writes)
        else:
            self.p.op(eng, lambda e: e.tensor_scalar(out, a, s1, s2, op0, op1), reads, writes)

    def stt(self, out, in0, scalar, in1, op0, op1, reads, writes, eng="vector"):
        self.p.op(eng, lambda e: e.scalar_tensor_tensor(out, in0, scalar, in1, op0, op1), reads, writes)

    def cp(self, out, a, reads, writes, eng="vector"):
        self.p.op(eng, lambda e: e.tensor_copy(out, a), reads, writes)

    def dma(self, out, in_, reads, writes, eng="sync", final=False):
        self.p.op(eng, lambda e: e.dma_start(out=out, in_=in_), reads, writes, dma=True, final=final)

    def bank(self):
        i = self.nbank % 8
        self.nbank += 1
        return self.ps[i], ("ps", i)

    def scr(self):
        i = self.nscr % self.NSCR
        self.nscr += 1
        return i, ("scr", i)

    def scr_f(self, i):
        return self.scrt[:, i, :]

    def scr_b(self, i):
        return self.scrt[:, i, :].bitcast(BF16)

    def add_item(self, w2d, kchunks, groups, compute):
        self.items.append((w2d, kchunks, groups, compute))

    def flush_items(self):
        NSL = self.NSLAB
        n = len(self.items)

        def load(i):
            w2d, kch, groups, _ = self.items[i]
            s = i % NSL
            off = 0
            for (c0, ncol) in groups:
                src = w2d[:, c0:c0 + ncol].rearrange("(kc p) c -> p kc c", p=128)
                self.dma(self.slab[s][:, 0:kch, off:off + ncol], src, [], [("slab", s)], eng="gpsimd")
                off += ncol
        for i in range(min(NSL - 1, n)):
            load(i)
        for i in range(n):
            if i + NSL - 1 < n:
                load(i + NSL - 1)
            self.items[i][3](self.slab[i % NSL], ("slab", i % NSL))
        self.items = []


def _common_decl(b, ntok, nslab, npar, own_mpart):
    b.ps = [b.st.enter_context(b.nc.psum_tensor("ps%d" % i, [128, 512], F32)) for i in range(8)]
    b.NSCR = 10
    b.scrt = b.sb("scrt", [128, b.NSCR, 512], F32)
    b.NSLAB = nslab
    b.slab = [b.sb("slab%d" % i, [128, 16, 256], BF16) for i in range(b.NSLAB)]
    b.xb = b.sb("xb", [128, 16, ntok], BF16)
    b.xlo = b.sb("xlo", [128, 16, ntok], BF16)
    b.ident = b.sb("ident", [128, 128], BF16)
    b.ones = b.sb("ones", [128, 128], BF16)
    b.rstd = b.sb("rstd", [128, ntok], F32)
    b.shift = b.sb("shift", [128, ntok], F32)
    b.lnp = b.sb("lnp", [128, 4, 2, 16], F32)
    b.mkT = [b.sb("mkT%d" % i, [128, 4, NMEM], BF16) for i in range(npar)]
    b.mv = [b.sb("mv%d" % i, [128, 2, 512], BF16) for i in range(npar)]
    b.ckT = [b.sb("ckT%d" % i, [128, 4, NMEM], BF16) for i in range(npar)]
    b.cv_ = [b.sb("cvv%d" % i, [128, 2, 512], BF16) for i in range(npar)]
    if own_mpart:
        b.mpart = b.sb("mpart", [128, 4, ntok], BF16)


def _load_x(b, xT, cts):
    for m in range(16):
        for ci, (c0, n) in enumerate(cts):
            si, sr = b.scr()
            t = b.scr_f(si)[:, 0:n]
            b.dma(t, xT[:, m, c0:c0 + n], [], [sr])
            b.act(b.xb[:, m, c0:c0 + n], t, AF.Copy, [sr], [("xb", m, ci)])
            b.tt(b.xlo[:, m, c0:c0 + n], t, b.xb[:, m, c0:c0 + n], ALU.subtract, [sr, ("xb", m, ci)], [("xlo", m, ci)])


def _stats(b, srcs, cts, nfeat, eps):
    inv = 1.0 / nfeat
    for ci, (c0, n) in enumerate(cts):
        lst = srcs(ci)
        sb_, sr_ = b.bank()
        qb_, qr_ = b.bank()
        nl = len(lst)
        for k, (ap, rn) in enumerate(lst):
            b.mm(sb_[:, 0:n], b.ones[:], ap, k == 0, k == nl - 1, [rn, "ones"], [sr_])
            si, sr = b.scr()
            sq = b.scr_b(si)[:, 0:n]
            b.act(sq, ap, AF.Square, [rn], [sr])
            b.mm(qb_[:, 0:n], b.ones[:], sq, k == 0, k == nl - 1, [sr, "ones"], [qr_])
        si, sr = b.scr()
        mean = b.scr_f(si)[:, 0:n]
        b.ts(mean, sb_[:, 0:n], inv, None, ALU.mult, None, [sr_], [sr])
        si2, sr2 = b.scr()
        msq = b.scr_f(si2)[:, 0:n]
        b.tt(msq, mean, mean, ALU.mult, [sr], [sr2])
        b.stt(msq, qb_[:, 0:n], inv, msq, ALU.mult, ALU.subtract, [qr_, sr2], [sr2])
        b.ts(msq, msq, eps, None, ALU.add, None, [sr2], [sr2])
        b.act(msq, msq, AF.Sqrt, [sr2], [sr2])
        b.p.op("vector", lambda e, o=b.rstd[:, c0:c0 + n], i=msq: e.reciprocal(o, i), [sr2], [("rstd", ci)])
        b.stt(b.shift[:, c0:c0 + n], mean, -1.0, b.rstd[:, c0:c0 + n], ALU.mult, ALU.mult, [sr, ("rstd", ci)], [("shift", ci)])


def _mem_kv(b, l, par, memT, w_mem_kv, memkv_o):
    w2d = w_mem_kv[l]
    for half in range(4):

        def comp(slab, sres, half=half):
            c0 = half * 256
            for tk in range(2):
                bk, br = b.bank()
                for kc in range(16):
                    b.mm(bk[:, 0:256], memT[:, kc, tk * 128:(tk + 1) * 128], slab[:, kc, 0:256], kc == 0, kc == 15,
                         ["memT", sres], [br])
                si, sr = b.scr()
                t = b.scr_f(si)[:, 0:256]
                b.act(t, bk[:, 0:256], AF.Copy, [br], [sr])
                if memkv_o is not None:
                    b.dma(memkv_o[l, tk * 128:(tk + 1) * 128, c0:c0 + 256], t, [sr], [], final=True)
                if half >= 2:
                    b.cp(b.mv[par][:, tk, c0 - 512:c0 - 512 + 256], t, [sr], [("mv", par)])
            if half < 2:
                for hh in range(2):
                    h = half * 2 + hh
                    bk, br = b.bank()
                    for kc in range(16):
                        b.mm(bk[:, 0:NMEM], slab[:, kc, hh * 128:(hh + 1) * 128], memT[:, kc, :], kc == 0, kc == 15,
                             ["memT", sres], [br])
                    b.act(b.mkT[par][:, h, :], bk[:, 0:NMEM], AF.Copy, [br], [("mkT", par)])
        b.add_item(w2d, 16, [(half * 256, 256)], comp)


def _mem_attn(b, par, h, qT, qres, sg, sgres, cts, ns_c0, ci, c0, n):
    segs = []
    pe = min(c0 + n, ns_c0)
    if pe > c0:
        segs.append((0, pe - c0, b.mkT[par], b.mv[par], ("mkT", par), ("mv", par)))
    if c0 + n > ns_c0:
        s0 = max(c0, ns_c0) - c0
        segs.append((s0, n - s0, b.ckT[par], b.cv_[par], ("ckT", par), ("cvv", par)))
    for (o, w, KT, V, kres, vres) in segs:
        pts = []
        for kc in range(2):
            bk, br = b.bank()
            b.mm(bk[:, 0:w], KT[:, h, kc * 128:(kc + 1) * 128], qT[:, o:o + w], True, True, [kres, qres], [br])
            si, sr = b.scr()
            pt = b.scr_b(si)[:, 0:w]
            b.act(pt, bk[:, 0:w], AF.Exp, [br], [sr], scale=SCALE)
            pts.append((pt, sr))
        nb_, nr_ = b.bank()
        db_, dr_ = b.bank()
        for kc in range(2):
            b.mm(nb_[:, 0:w], V[:, kc, h * 128:(h + 1) * 128], pts[kc][0], kc == 0, kc == 1, [vres, pts[kc][1]], [nr_])
        for kc in range(2):
            b.mm(db_[:, 0:w], b.ones[:], pts[kc][0], kc == 0, kc == 1, ["ones", pts[kc][1]], [dr_])
        si, sr = b.scr()
        rd = b.scr_f(si)[:, 0:w]
        b.p.op("vector", lambda e, rd=rd, db_=db_, w=w: e.reciprocal(rd, db_[:, 0:w]), [dr_], [sr])
        b.tt(rd, nb_[:, 0:w], rd, ALU.mult, [nr_, sr], [sr])
        b.tt(b.mpart[:, h, c0 + o:c0 + o + w], rd, sg[:, o:o + w], ALU.mult, [sr, sgres], [("mp", h, ci)])


def _out_ln(b, l, w_out2d, nk, cm_src, cts, last_out=None):
    for mq in range(8):

        def comp(slab, sres, mq=mq):
            for mo in range(2):
                m = mq * 2 + mo
                for ci, (c0, n) in enumerate(cts):
                    bk, br = b.bank()
                    for kc in range(nk):
                        ap, rn = cm_src(kc, ci, c0, n)
                        b.mm(bk[:, 0:n], slab[:, kc, mo * 128:(mo + 1) * 128], ap, kc == 0, False, [sres, rn], [br])
                    b.mm(bk[:, 0:n], b.ident[:], b.xb[:, m, c0:c0 + n], False, False, ["ident", ("xb", m, ci)], [br])
                    b.mm(bk[:, 0:n], b.ident[:], b.xlo[:, m, c0:c0 + n], False, True, ["ident", ("xlo", m, ci)], [br])
                    b.act(b.xb[:, m, c0:c0 + n], bk[:, 0:n], AF.Copy, [br], [("xb", m, ci)])
                    b.tt(b.xlo[:, m, c0:c0 + n], bk[:, 0:n], b.xb[:, m, c0:c0 + n], ALU.subtract,
                         [br, ("xb", m, ci)], [("xlo", m, ci)])
        b.add_item(w_out2d, nk, [(mq * 256, 256)], comp)
    b.flush_items()
    _stats(b, lambda ci: [(b.xb[:, m, cts[ci][0]:cts[ci][0] + cts[ci][1]], ("xb", m, ci)) for m in range(16)], cts, D, EPS_Z)
    for m in range(16):
        for ci, (c0, n) in enumerate(cts):
            si, sr = b.scr()
            s = b.scr_f(si)[:, 0:n]
            b.tt(s, b.xb[:, m, c0:c0 + n], b.xlo[:, m, c0:c0 + n], ALU.add, [("xb", m, ci), ("xlo", m, ci)], [sr])
            b.tt(s, s, b.rstd[:, c0:c0 + n], ALU.mult, [sr, ("rstd", ci)], [sr])
            b.tt(s, s, b.shift[:, c0:c0 + n], ALU.add, [sr, ("shift", ci)], [sr])
            b.act(s, s, AF.Identity, [sr, "lnp"], [sr], bias=b.lnp[:, l, 1, m:m + 1], scale=b.lnp[:, l, 0, m:m + 1])
            if last_out is not None:
                b.dma(last_out[:, m, c0:c0 + n], s, [sr], [], final=True)
            b.cp(b.xb[:, m, c0:c0 + n], s, [sr], [("xb", m, ci)], eng="gpsimd")
            b.tt(b.xlo[:, m, c0:c0 + n], s, b.xb[:, m, c0:c0 + n], ALU.subtract, [sr, ("xb", m, ci)], [("xlo", m, ci)])


CTA = [(0, 512), (512, 512), (1024, 72)]
CONV_LO = 32


def build_A():
    b = Builder("A")
    nc = b.nc
    xT = b.din("xT", [128, 16, NTA])
    memT_d = b.din("memT", [128, 16, NMEM])
    ident_d = b.din("identf", [128, 128])
    hmask_d = b.din("hmask", [128, 1])
    a_w_in = b.din("a_w_in", [2, D, A_IN])
    a_w_out = b.din("a_w_out", [2, D, D])
    w_kv = b.din("w_kv", [D, 3072])
    w_mem_kv = b.din("w_mem_kv", [4, D, 1024])
    wdw_d = b.din("wdw", [128, 2, NJ, KW])
    cnp_d = b.din("cnp", [128, 2, 3, NJ])
    lnp_d = b.din("lnp", [128, 4, 2, 16])
    convst_d = b.din("convst", [128, 2, NJ, 30])
    convst_nat = b.din("convst_nat", [2, 30, CC])
    ckT_d = b.din("ckT", [128, 4, 4, NMEM])
    cv_d = b.din("cv", [128, 4, 2, 512])

    memkv_o = b.dout("memkv_o", [4, NMEM, 1024])
    convp_o = b.dout("convp_o", [128, 2, NJ, 30])
    convsn_o = b.dout("convsn_o", [128, 2, NJ, NS])
    convss_o = b.dout("convss_o", [2, 22, CC])
    kT_o = b.dout("kT_o", [128, 12, NTB])
    v_o = b.dout("v_o", [NTB, CC])
    x2_o = b.dout("x2_o", [128, 16, NTB])
    ks_o = b.dout("ks_o", [NS, CC])
    skv_d = [b.din("skv%d" % g, [GROUPS[g][0], 1024]) for g in range(3)]
    skv_o = [b.dout("skv_o%d" % g, [GROUPS[g][0] - NS, 1024]) for g in range(3)]

    _common_decl(b, NTA, 3, 2, True)
    ring = b.sb("ring", [128, NJ + 1, NTA], BF16)
    exts = b.sb("exts", [128, NJ, 30 + NS], BF16)
    memT = b.sb("memTs", [128, 16, NMEM], BF16)
    hmask = b.sb("hmask_s", [128, 1], F32)
    wdw = b.sb("wdw_s", [128, 2, NJ, KW], F32)
    cnp = b.sb("cnp_s", [128, 2, 3, NJ], F32)
    NDG = 40
    dg = b.sb("dg", [128, NDG, 128], BF16)
    convo = b.sb("convo", [128, 2, NJ, 30], F32)
    convsn = b.sb("convsn", [128, 2, NJ, NS], F32)

    b.dma(b.ident[:], ident_d, [], ["ident"], eng="gpsimd")
    b.p.op("vector", lambda e: e.memset(b.ones[:], 1.0), [], ["ones"])
    b.p.op("vector", lambda e: e.memset(ring[:], 0.0), [], [("ring", s, ci) for s in range(NJ + 1) for ci in range(3)])
    b.p.op("vector", lambda e: e.memset(b.mpart[:], 0.0), [], [("mp", h, ci) for h in range(4) for ci in range(3)])
    b.dma(hmask[:], hmask_d, [], ["hmask"])
    b.dma(wdw[:], wdw_d, [], ["wdw"])
    b.dma(cnp[:], cnp_d, [], ["cnp"])
    b.dma(b.lnp[:], lnp_d, [], ["lnp"])
    b.dma(memT[:], memT_d, [], ["memT"], eng="gpsimd")
    for l in range(2):
        b.dma(convss_o[l], convst_nat[l, 8:30, :], [], [], final=True)
    for g in range(3):
        b.dma(skv_o[g], skv_d[g][NS:, :], [], [], final=True)
    _load_x(b, xT, CTA)

    for l in range(4):
        _mem_kv(b, l, l % 2, memT, w_mem_kv, memkv_o)
        if l == 1:
            b.flush_items()
    late_items = b.items
    b.items = []

    for l in range(2):
        par = l % 2
        w_in = a_w_in[l]
        b.dma(b.ckT[par][:], ckT_d[:, l], [], [("ckT", par)], eng="gpsimd")
        b.dma(b.cv_[par][:], cv_d[:, l], [], [("cvv", par)], eng="gpsimd")
        b.dma(exts[:, :, 0:30], convst_d[:, l], [], [("exts", j) for j in range(NJ)], eng="gpsimd")

        for j in range(NJ):
            def comp(slab, sres, j=j, l=l):
                for ci, (c0, n) in enumerate(CTA):
                    ba, ra = b.bank()
                    bb, rb = b.bank()
                    for kc in range(16):
                        b.mm(ba[:, 0:n], slab[:, kc, 0:128], b.xb[:, kc, c0:c0 + n], kc == 0, kc == 15,
                             [sres, ("xb", kc, ci)], [ra])
                    for kc in range(16):
                        b.mm(bb[:, 0:n], slab[:, kc, 128:256], b.xb[:, kc, c0:c0 + n], kc == 0, kc == 15,
                             [sres, ("xb", kc, ci)], [rb])
                    si, sr = b.scr()
                    sg = b.scr_f(si)[:, 0:n]
                    b.act(sg, bb[:, 0:n], AF.Sigmoid, [rb], [sr])
                    rr = ("ring", j + 1, ci)
                    b.tt(ring[:, j + 1, c0:c0 + n], ba[:, 0:n], sg, ALU.mult, [ra, sr], [rr])
                    if ci == 0:
                        b.ts(ring[:, j + 1, 0:HALO], ring[:, j + 1, 0:HALO], hmask[:, 0:1], None, ALU.mult, None,
                             [rr, "hmask"], [rr])
                    if ci == 2:
                        b.tt(convo[:, l, j, :], ba[:, 34:64], sg[:, 34:64], ALU.mult, [ra, sr], [("convo", l)])
                        b.tt(convsn[:, l, j, :], ba[:, 64:72], sg[:, 64:72], ALU.mult, [ra, sr], [("convsn", l)])
                        b.tt(exts[:, j, 30:38], ba[:, 64:72], sg[:, 64:72], ALU.mult, [ra, sr], [("exts", j)])
            b.add_item(w_in, 16, [(j * 128, 128), (CC + j * 128, 128)], comp)
        b.flush_items()
        b.dma(convp_o[:, l], convo[:, l], [("convo", l)], [], final=True)
        b.dma(convsn_o[:, l], convsn[:, l], [("convsn", l)], [], final=True)

        conv_cts = [(CONV_LO, 512 - CONV_LO), (512, 512), (1024, 64)]
        for j in range(NJ):
            dgs = []
            for k in range(KW):
                di = b.ndg % NDG
                b.ndg += 1
                b.ts(dg[:, di, :], b.ident[:], wdw[:, l, j, k:k + 1], None, ALU.mult, None, ["ident", "wdw"], [("dg", di)])
                dgs.append(di)
            for ci, (c0, n) in enumerate(conv_cts):
                bk, br = b.bank()
                rds = [("ring", j + 1, ci)] + ([("ring", j + 1, ci - 1)] if ci > 0 else [])
                for k in range(KW):
                    b.mm(bk[:, 0:n], dg[:, dgs[k], :], ring[:, j + 1, c0 + k - 30:c0 + k - 30 + n], k == 0, k == KW - 1,
                         [("dg", dgs[k])] + rds, [br])
                b.act(ring[:, j, c0:c0 + n], bk[:, 0:n], AF.Identity, [br, "cnp"], [("ring", j, ci)], bias=cnp[:, l, 0, j:j + 1])
            bk, br = b.bank()
            for k in range(KW):
                b.mm(bk[:, 0:NS], dg[:, dgs[k], :], exts[:, j, k:k + NS], k == 0, k == KW - 1, [("dg", dgs[k]), ("exts", j)], [br])
            b.act(ring[:, j, HALO + CH:NTA], bk[:, 0:NS], AF.Identity, [br, "cnp"], [("ring", j, 2)], bias=cnp[:, l, 0, j:j + 1])

        _stats(b, lambda ci: [(ring[:, j, CTA[ci][0]:CTA[ci][0] + CTA[ci][1]], ("ring", j, ci)) for j in range(NJ)], CTA, CC, LN_EPS)
        for jq in range(NJ // 2):
            def comp(slab, sres, jq=jq, l=l):
                for jo in range(2):
                    j = jq * 2 + jo
                    for ci, (c0, n) in enumerate(CTA):
                        bk, br = b.bank()
                        for kc in range(16):
                            b.mm(bk[:, 0:n], slab[:, kc, jo * 128:(jo + 1) * 128], b.xb[:, kc, c0:c0 + n], kc == 0, kc == 15,
                                 [sres, ("xb", kc, ci)], [br])
                        si, sr = b.scr()
                        sg = b.scr_f(si)[:, 0:n]
                        b.act(sg, bk[:, 0:n], AF.Silu, [br], [sr])
                        si2, sr2 = b.scr()
                        t = b.scr_f(si2)[:, 0:n]
                        rr = ("ring", j, ci)
                        b.tt(t, ring[:, j, c0:c0 + n], b.rstd[:, c0:c0 + n], ALU.mult, [rr, ("rstd", ci)], [sr2])
                        b.tt(t, t, b.shift[:, c0:c0 + n], ALU.add, [sr2, ("shift", ci)], [sr2])
                        b.act(t, t, AF.Silu, [sr2, "cnp"], [sr2], bias=cnp[:, l, 2, j:j + 1], scale=cnp[:, l, 1, j:j + 1])
                        b.stt(ring[:, j, c0:c0 + n], t, INV_ALPHA, sg, ALU.mult, ALU.mult, [sr2, sr], [rr])
            b.add_item(w_in, 16, [(2 * CC + jq * 256, 256)], comp)

        for h in range(4):
            def comp(slab, sres, h=h, par=par):
                for ci, (c0, n) in enumerate(CTA):
                    bq, rq = b.bank()
                    bg, rg = b.bank()
                    for kc in range(16):
                        b.mm(bq[:, 0:n], slab[:, kc, 0:128], b.xb[:, kc, c0:c0 + n], kc == 0, kc == 15, [sres, ("xb", kc, ci)], [rq])
                    for kc in range(16):
                        b.mm(bg[:, 0:n], slab[:, kc, 128:256], b.xb[:, kc, c0:c0 + n], kc == 0, kc == 15, [sres, ("xb", kc, ci)], [rg])
                    si, sr = b.scr()
                    qT = b.scr_b(si)[:, 0:n]
                    b.act(qT, bq[:, 0:n], AF.Copy, [rq], [sr])
                    si2, sr2 = b.scr()
                    sg = b.scr_b(si2)[:, 0:n]
                    b.act(sg, bg[:, 0:n], AF.Silu, [rg], [sr2])
                    b.ts(sg, sg, INV_ALPHA, None, ALU.mult, None, [sr2], [sr2])
                    _mem_attn(b, par, h, qT, sr, sg, sr2, CTA, HALO + CH, ci, c0, n)
            b.add_item(w_in, 16, [(3 * CC + h * 128, 128), (3 * CC + 512 + h * 128, 128)], comp)
        b.flush_items()

        def cm_src(kc, ci, c0, n):
            if kc < NJ:
                return ring[:, kc, c0:c0 + n], ("ring", kc, ci)
            return b.mpart[:, kc - NJ, c0:c0 + n], ("mp", kc - NJ, ci)
        _out_ln(b, l, a_w_out[l], 16, cm_src, CTA)

    for m in range(16):
        for (c0, n, cis) in [(HALO, 512, (0, 1)), (HALO + 512, 512, (1, 2)), (HALO + CH, NS, (2,))]:
            si, sr = b.scr()
            t = b.scr_f(si)[:, 0:n]
            rds = [("xb", m, c) for c in cis] + [("xlo", m, c) for c in cis]
            b.tt(t, b.xb[:, m, c0:c0 + n], b.xlo[:, m, c0:c0 + n], ALU.add, rds, [sr])
            b.dma(x2_o[:, m, c0 - HALO:c0 - HALO + n], t, [sr], [], final=True)

    cts_kv = [(HALO, 512), (HALO + 512, 512), (HALO + CH, NS)]
    cis_of = {0: [0, 1], 1: [1, 2], 2: [2]}
    for fq in range(6):
        def comp(slab, sres, fq=fq):
            for fo in range(2):
                f = fq * 2 + fo
                for ti, (c0, n) in enumerate(cts_kv):
                    bk, br = b.bank()
                    rds = [("xb", kc, c) for kc in range(16) for c in cis_of[ti]]
                    for kc in range(16):
                        b.mm(bk[:, 0:n], slab[:, kc, fo * 128:(fo + 1) * 128], b.xb[:, kc, c0:c0 + n], kc == 0, kc == 15,
                             [sres] + (rds if kc == 0 else []), [br])
                    si, sr = b.scr()
                    t = b.scr_f(si)[:, 0:n]
                    b.act(t, bk[:, 0:n], AF.Copy, [br], [sr])
                    b.dma(kT_o[:, f, c0 - HALO:c0 - HALO + n], t, [sr], [], final=True)
            bk, br = b.bank()
            for kc in range(16):
                b.mm(bk[0:NS, 0:256], b.xb[:, kc, HALO + CH:NTA], slab[:, kc, 0:256], kc == 0, kc == 15,
                     [sres] + ([("xb", k2, 2) for k2 in range(16)] if kc == 0 else []), [br])
            si, sr = b.scr()
            t = b.scr_f(si)[0:NS, 0:256]
            b.act(t, bk[0:NS, 0:256], AF.Copy, [br], [sr])
            b.dma(ks_o[:, fq * 256:(fq + 1) * 256], t, [sr], [], final=True)
        b.add_item(w_kv, 16, [(fq * 256, 256)], comp)
    for vq in range(6):
        def comp(slab, sres, vq=vq):
            tiles = [(HALO + 128 * t, 128) for t in range(8)] + [(HALO + CH, NS)]
            for (c0, n) in tiles:
                bk, br = b.bank()
                ci = min((c0) // 512, 2)
                for kc in range(16):
                    b.mm(bk[0:n, 0:256], b.xb[:, kc, c0:c0 + n], slab[:, kc, 0:256], kc == 0, kc == 15,
                         [sres] + ([("xb", k2, c) for k2 in range(16) for c in (ci, min(ci + 1, 2))] if kc == 0 else []), [br])
                si, sr = b.scr()
                t = b.scr_f(si)[0:n, 0:256]
                b.act(t, bk[0:n, 0:256], AF.Copy, [br], [sr])
                b.dma(v_o[c0 - HALO:c0 - HALO + n, vq * 256:(vq + 1) * 256], t, [sr], [], final=True)
        b.add_item(w_kv, 16, [(CC + vq * 256, 256)], comp)
    b.items = b.items + late_items
    b.flush_items()
    b.p.emit(nc, b.st)
    return b


CTB = [(0, 512), (512, 512), (1024, NS)]
LQ = [CH // d for (_, d) in GROUPS]
NK = [128 + q for q in LQ]
NCH = [(k + 127) // 128 for k in NK]
NST = [w // 128 for (w, _) in GROUPS]
LS = [w + 16 for (w, _) in GROUPS]
OGS = [128, 128 + LS[0], 128 + LS[0] + LS[1]]
TABS = 128 + sum(LS)
NEG = -30000.0


def build_B():
    b = Builder("B")
    nc = b.nc
    x2T = b.din("x2T", [128, 16, NTB])
    ident_d = b.din("identf", [128, 128])
    b_w_in = b.din("b_w_in", [2, D, B_IN])
    b_w_out = b.din("b_w_out", [2, 1024, D])
    lnp_d = b.din("lnp", [128, 4, 2, 16])
    mkT_d = b.din("mkT", [128, 2, 4, NMEM])
    mv_d = b.din("mv", [128, 2, 2, 512])
    ckT_d = b.din("ckT", [128, 4, 4, NMEM])
    cv_d = b.din("cv", [128, 4, 2, 512])
    relb_d = b.din("relb", [32, 12])
    ohp_d = b.din("ohp", [3, 33, 384])
    ohs_d = b.din("ohs", [33, sum(LS)])
    KT_d = [b.din("KT%d" % g, [128, 4, GROUPS[g][1], NK[g]]) for g in range(3)]
    V_d = [b.din("V%d" % g, [128, GROUPS[g][1], NCH[g], 512]) for g in range(3)]
    kval_d = [b.din("kval%d" % g, [128, GROUPS[g][1] * NCH[g]]) for g in range(3)]
    sKT_d = [b.din("sKT%d" % g, [128, 4, GROUPS[g][0]]) for g in range(3)]
    sV_d = [b.din("sV%d" % g, [128, NST[g], 512]) for g in range(3)]
    sKTn_d = b.din("sKTn", [128, 12, NS])
    sVn_d = b.din("sVn", [NS, CC])
    y_o = b.dout("y_o", [128, 16, NTB])
    tabp_h = nc.dram_tensor("tabp_scr", [12, 384], F32)
    tabs_h = nc.dram_tensor("tabs_scr", [4, TABS], F32)

    _common_decl(b, NTB, 2, 1, False)
    om = b.sb("om", [128, 8, NTB], BF16)
    b.mpart = om[:, 4:8, :]
    relb = b.sb("relb", [33, 12], F32)
    E = b.sb("E", [128, 12, 256], BF16)
    E2 = b.sb("E2", [64, 4, 256], BF16)
    Es = [b.sb("Es%d" % g, [128, 4, NST[g] + 1, NS], F32) for g in range(3)]
    kval = [b.sb("kval%d" % g, [128, GROUPS[g][1] * NCH[g]], F32) for g in range(3)]
    qT = [b.sb("qT%d" % g, [128, CH], BF16) for g in range(3)]
    qTs = b.sb("qTs", [128, 3, NS], BF16)
    sgate = b.sb("sgate", [128, NTB], BF16)
    kt = [b.sb("kt%d" % g, [128, GROUPS[g][1], NK[g]], BF16) for g in range(3)]
    vt = [b.sb("vt%d" % g, [128, GROUPS[g][1], NCH[g], 128], BF16) for g in range(3)]
    skt = [b.sb("skt%d" % g, [128, GROUPS[g][0] + NS], BF16) for g in range(3)]
    sv = [b.sb("sv%d" % g, [128, NST[g] + 1, 128], BF16) for g in range(3)]
    NUM = b.sb("NUM", [128, CH], F32)
    DEN = b.sb("DEN", [128, CH], F32)

    b.dma(b.ident[:], ident_d, [], ["ident"], eng="gpsimd")
    b.p.op("vector", lambda e: e.memset(b.ones[:], 1.0), [], ["ones"])
    b.p.op("vector", lambda e: e.memset(om[:], 0.0), [], [("om", k, ci) for k in range(4) for ci in range(3)] +
           [("mp", k, ci) for k in range(4) for ci in range(3)])
    b.dma(b.lnp[:], lnp_d, [], ["lnp"])
    b.p.op("vector", lambda e: e.memset(relb[32:33, :], NEG), [], ["relb"])
    b.dma(relb[0:32, :], relb_d, ["relb"], ["relb"])
    for g in range(3):
        b.dma(kval[g][:], kval_d[g], [], [("kval", g)])
    _load_x(b, x2T, CTB)

    for g in range(3):
        si, sr = b.scr()
        oh = b.scr_f(si)[0:33, 0:384]
        b.dma(oh, ohp_d[g], [], [sr])
        bk, br = b.bank()
        b.mm(bk[0:4, 0:384], relb[0:33, g * 4:(g + 1) * 4], oh, True, True, ["relb", sr], [br])
        si2, sr2 = b.scr()
        t = b.scr_f(si2)[0:4, 0:384]
        b.cp(t, bk[0:4, 0:384], [br], [sr2])
        b.dma(tabp_h.ap()[g * 4:(g + 1) * 4, :], t, [sr2], ["tabp"])
    for gh in range(12):
        si, sr = b.scr()
        t = b.scr_f(si)[:, 0:256]
        src = bass.AP(tensor=tabp_h, offset=gh * 384, ap=[[1, 128], [1, 256]])
        b.dma(t, src, ["tabp"], [sr])
        b.act(E[:, gh, :], t, AF.Exp, [sr], [("E", gh)])
    for h in range(4):
        si, sr = b.scr()
        t = b.scr_f(si)[0:64, 0:256]
        src = bass.AP(tensor=tabp_h, offset=(8 + h) * 384 + 64, ap=[[1, 64], [1, 256]])
        b.dma(t, src, ["tabp"], [sr])
        b.act(E2[0:64, h, :], t, AF.Exp, [sr], [("E2", h)])
    off = 0
    for g in range(3):
        for pc in range(0, LS[g], 512):
            n = min(512, LS[g] - pc)
            si, sr = b.scr()
            oh = b.scr_f(si)[0:33, 0:n]
            b.dma(oh, ohs_d[:, off + pc:off + pc + n], [], [sr])
            bk, br = b.bank()
            b.mm(bk[0:4, 0:n], relb[0:33, g * 4:(g + 1) * 4], oh, True, True, ["relb", sr], [br])
            si2, sr2 = b.scr()
            t = b.scr_f(si2)[0:4, 0:n]
            b.cp(t, bk[0:4, 0:n], [br], [sr2])
            b.dma(tabs_h.ap()[:, OGS[g] + pc:OGS[g] + pc + n], t, [sr2], ["tabs"])
        off += LS[g]
    for g in range(3):
        W = GROUPS[g][0]
        for h in range(4):
            src = bass.AP(tensor=tabs_h, offset=h * TABS + OGS[g] + 9, ap=[[1, 128], [128, NST[g]], [1, NS]])
            b.dma(Es[g][:, h, 0:NST[g], :], src, ["tabs"], [("Es", g, h)])
            src2 = bass.AP(tensor=tabs_h, offset=h * TABS + OGS[g] + 1, ap=[[1, NS], [1, NS]])
            b.dma(Es[g][0:NS, h, NST[g], :], src2, ["tabs"], [("Es", g, h)])
            b.act(Es[g][:, h, 0:NST[g], :], Es[g][:, h, 0:NST[g], :], AF.Exp, [("Es", g, h)], [("Es", g, h)])
            b.act(Es[g][0:NS, h, NST[g], :], Es[g][0:NS, h, NST[g], :], AF.Exp, [("Es", g, h)], [("Es", g, h)])

    def load_kv(h):
        for g in range(3):
            d = GROUPS[g][1]
            W = GROUPS[g][0]
            b.dma(kt[g][:], KT_d[g][:, h], [], [("kt", g)], eng="gpsimd")
            b.dma(vt[g][:], V_d[g][:, :, :, h * 128:(h + 1) * 128], [], [("vt", g)], eng="gpsimd")
            b.dma(skt[g][:, 0:W], sKT_d[g][:, h, :], [], [("skt", g)], eng="gpsimd")
            b.dma(skt[g][:, W:W + NS], sKTn_d[:, g * 4 + h, :], [], [("skt", g)], eng="gpsimd")
            b.dma(sv[g][:, 0:NST[g], :], sV_d[g][:, :, h * 128:(h + 1) * 128], [], [("sv", g)], eng="gpsimd")
            b.dma(sv[g][0:NS, NST[g], :], sVn_d[:, (g * 4 + h) * 128:(g * 4 + h + 1) * 128], [], [("sv", g)], eng="gpsimd")

    def dil_attn(h):
        pending = []

        def flush_pv(keep):
            while len(pending) > keep:
                pending.pop(0)()

        def emit_pv(g, c, qb, pts):
            d = GROUPS[g][1]
            lq = LQ[g]
            q0 = 128 * qb
            nqb = min(128, lq - q0)
            contrib = [ii for ii in (qb, qb + 1) if ii in pts]
            nb_, nr_ = b.bank()
            db_, dr_ = b.bank()
            for k, ii in enumerate(contrib):
                ptf, prs, pqs, pks = pts[ii]
                b.mm(nb_[:, 0:nqb], vt[g][0:pks, c, ii, :], ptf[0:pks, q0 - pqs:q0 - pqs + nqb], k == 0,
                     k == len(contrib) - 1, [("vt", g), prs], [nr_])
            for k, ii in enumerate(contrib):
                ptf, prs, pqs, pks = pts[ii]
                b.mm(db_[:, 0:nqb], b.ones[0:pks, :], ptf[0:pks, q0 - pqs:q0 - pqs + nqb], k == 0,
                     k == len(contrib) - 1, ["ones", prs], [dr_])
            t0 = d * q0 + c
            nsl = NUM[:, t0:t0 + d * (nqb - 1) + 1:d]
            dsl = DEN[:, t0:t0 + d * (nqb - 1) + 1:d]
            if g == 0:
                b.act(nsl, nb_[:, 0:nqb], AF.Copy, [nr_], ["NUM"])
                b.cp(dsl, db_[:, 0:nqb], [dr_], ["DEN"])
            else:
                b.tt(nsl, nb_[:, 0:nqb], nsl, ALU.add, [nr_, "NUM"], ["NUM"])
                b.tt(dsl, db_[:, 0:nqb], dsl, ALU.add, [dr_, "DEN"], ["DEN"])

        for g in range(3):
            d = GROUPS[g][1]
            lq, nch = LQ[g], NCH[g]
            gh = g * 4 + h
            for c in range(d):
                pts = {}
                for i in range(nch):
                    klo = -128 + 128 * i
                    ks = min(128, NK[g] - 128 * i)
                    qs = max(0, klo)
                    qe = min(lq, klo + ks + 128)
                    nq = qe - qs
                    if nq > 0:
                        bk, br = b.bank()
                        b.mm(bk[0:ks, 0:nq], kt[g][:, c, 128 * i:128 * i + ks], qT[g][:, c * lq + qs:c * lq + qe], True, True,
                             [("kt", g), ("qT", g)], [br])
                        si, sr = b.scr()
                        tmp = b.scr_f(si)[0:ks, 0:nq]
                        b.act(tmp, bk[0:ks, 0:nq], AF.Exp, [br], [sr], scale=SCALE)
                        si2, sr2 = b.scr()
                        pt = b.scr_b(si2)[0:ks, 0:nq]
                        if ks == 128:
                            eap, eres = E[:, gh, qs - klo:qs - klo + nq], ("E", gh)
                        else:
                            assert ks == 64 and g == 2
                            eap, eres = E2[0:64, h, qs - klo:qs - klo + nq], ("E2", h)
                        b.stt(pt, tmp, kval[g][0:ks, c * nch + i:c * nch + i + 1], eap,
                              ALU.mult, ALU.mult, [sr, ("kval", g), eres], [sr2])
                        pts[i] = (b.scr_b(si2), sr2, qs, ks)
                    qb = i - 1
                    if qb >= 0 and 128 * qb < lq:
                        pending.append(lambda g=g, c=c, qb=qb, pts=pts: emit_pv(g, c, qb, pts))
                    flush_pv(1)
        flush_pv(0)
        for ci, (c0, n) in enumerate(CTB[0:2]):
            si, sr = b.scr()
            rd = b.scr_f(si)[:, 0:n]
            b.p.op("vector", lambda e, rd=rd, c0=c0, n=n: e.reciprocal(rd, DEN[:, c0:c0 + n]), ["DEN"], [sr])
            b.tt(rd, NUM[:, c0:c0 + n], rd, ALU.mult, ["NUM", sr], [sr])
            b.tt(om[:, h, c0:c0 + n], rd, sgate[:, c0:c0 + n], ALU.mult, [sr, "sgate"], [("om", h, ci)])
        nb_, nr_ = b.bank()
        db_, dr_ = b.bank()
        first = True
        for g in range(3):
            W = GROUPS[g][0]
            nst = NST[g]
            pieces = []
            bk, br = b.bank()
            for ch in range(nst):
                b.mm(bk[:, ch * NS:(ch + 1) * NS], skt[g][:, ch * 128:(ch + 1) * 128], qTs[:, g, :], True, True,
                     [("skt", g), "qTs"], [br])
            si, sr = b.scr()
            tmp = b.scr_f(si)[:, 0:nst * NS]
            b.act(tmp, bk[:, 0:nst * NS], AF.Exp, [br], [sr], scale=SCALE)
            si2, sr2 = b.scr()
            pt = b.scr_b(si2)[:, 0:nst * NS]
            b.tt(pt.rearrange("p (c q) -> p c q", q=NS), tmp.rearrange("p (c q) -> p c q", q=NS), Es[g][:, h, 0:nst, :],
                 ALU.mult, [sr, ("Es", g, h)], [sr2])
            for ch in range(nst):
                pieces.append((pt[:, ch * NS:(ch + 1) * NS], sr2, 128, ch))
            bk, br = b.bank()
            b.mm(bk[0:NS, 0:NS], skt[g][:, W:W + NS], qTs[:, g, :], True, True, [("skt", g), "qTs"], [br])
            si, sr = b.scr()
            tmp = b.scr_f(si)[0:NS, 0:NS]
            b.act(tmp, bk[0:NS, 0:NS], AF.Exp, [br], [sr], scale=SCALE)
            si3, sr3 = b.scr()
            pt2 = b.scr_b(si3)[0:NS, 0:NS]
            b.tt(pt2, tmp, Es[g][0:NS, h, nst, :], ALU.mult, [sr, ("Es", g, h)], [sr3])
            pieces.append((pt2, sr3, NS, nst))
            for k, (pp, prs, ks, ch) in enumerate(pieces):
                last = (g == 2 and k == len(pieces) - 1)
                b.mm(nb_[:, 0:NS], sv[g][0:ks, ch, :], pp, first, last, [("sv", g), prs], [nr_])
                b.mm(db_[:, 0:NS], b.ones[0:ks, :], pp, first, last, ["ones", prs], [dr_])
                first = False
        si, sr = b.scr()
        rd = b.scr_f(si)[:, 0:NS]
        b.p.op("vector", lambda e, rd=rd, db_=db_: e.reciprocal(rd, db_[:, 0:NS]), [dr_], [sr])
        b.tt(rd, nb_[:, 0:NS], rd, ALU.mult, [nr_, sr], [sr])
        b.tt(om[:, h, CH:NTB], rd, sgate[:, CH:NTB], ALU.mult, [sr, "sgate"], [("om", h, 2)])

    for l in (2, 3):
        i_ = l - 2
        w_in = b_w_in[i_]
        b.dma(b.mkT[0][:], mkT_d[:, i_], [], [("mkT", 0)], eng="gpsimd")
        b.dma(b.mv[0][:], mv_d[:, i_], [], [("mv", 0)], eng="gpsimd")
        b.dma(b.ckT[0][:], ckT_d[:, l], [], [("ckT", 0)], eng="gpsimd")
        b.dma(b.cv_[0][:], cv_d[:, l], [], [("cvv", 0)], eng="gpsimd")
        for h in range(4):
            def comp1(slab, sres, h=h):
                load_kv(h)
                for g in (0, 1):
                    d = GROUPS[g][1]
                    for ci, (c0, n) in enumerate(CTB):
                        bk, br = b.bank()
                        for kc in range(16):
                            b.mm(bk[:, 0:n], slab[:, kc, g * 128:(g + 1) * 128], b.xb[:, kc, c0:c0 + n], kc == 0, kc == 15,
                                 [sres, ("xb", kc, ci)], [br])
                        if ci < 2:
                            dst = qT[g][:, :].rearrange("p (c m) -> p c m", c=d)[:, :, c0 // d:(c0 + n) // d]
                            src = bk[:, 0:n].rearrange("p (m c) -> p c m", c=d)
                            b.act(dst, src, AF.Copy, [br], [("qT", g)])
                        else:
                            b.act(qTs[:, g, :], bk[:, 0:n], AF.Copy, [br], ["qTs"])

            def comp2(slab, sres, h=h):
                g = 2
                d = GROUPS[g][1]
                for ci, (c0, n) in enumerate(CTB):
                    bk, br = b.bank()
                    for kc in range(16):
                        b.mm(bk[:, 0:n], slab[:, kc, 0:128], b.xb[:, kc, c0:c0 + n], kc == 0, kc == 15,
                             [sres, ("xb", kc, ci)], [br])
                    if ci < 2:
                        dst = qT[g][:, :].rearrange("p (c m) -> p c m", c=d)[:, :, c0 // d:(c0 + n) // d]
                        src = bk[:, 0:n].rearrange("p (m c) -> p c m", c=d)
                        b.act(dst, src, AF.Copy, [br], [("qT", g)])
                    else:
                        b.act(qTs[:, g, :], bk[:, 0:n], AF.Copy, [br], ["qTs"])
                    bg, rg = b.bank()
                    for kc in range(16):
                        b.mm(bg[:, 0:n], slab[:, kc, 128:256], b.xb[:, kc, c0:c0 + n], kc == 0, kc == 15,
                             [sres, ("xb", kc, ci)], [rg])
                    b.act(sgate[:, c0:c0 + n], bg[:, 0:n], AF.Silu, [rg], ["sgate"])
                    b.ts(sgate[:, c0:c0 + n], sgate[:, c0:c0 + n], INV_ALPHA, None, ALU.mult, None, ["sgate"], ["sgate"])
                dil_attn(h)

            def comp3(slab, sres, h=h):
                for ci, (c0, n) in enumerate(CTB):
                    bq, rq = b.bank()
                    bg, rg = b.bank()
                    for kc in range(16):
                        b.mm(bq[:, 0:n], slab[:, kc, 0:128], b.xb[:, kc, c0:c0 + n], kc == 0, kc == 15, [sres, ("xb", kc, ci)], [rq])
                    for kc in range(16):
                        b.mm(bg[:, 0:n], slab[:, kc, 128:256], b.xb[:, kc, c0:c0 + n], kc == 0, kc == 15, [sres, ("xb", kc, ci)], [rg])
                    si, sr = b.scr()
                    qm = b.scr_b(si)[:, 0:n]
                    b.act(qm, bq[:, 0:n], AF.Copy, [rq], [sr])
                    si2, sr2 = b.scr()
                    sg = b.scr_b(si2)[:, 0:n]
                    b.act(sg, bg[:, 0:n], AF.Silu, [rg], [sr2])
                    b.ts(sg, sg, INV_ALPHA, None, ALU.mult, None, [sr2], [sr2])
                    _mem_attn(b, 0, h, qm, sr, sg, sr2, CTB, CH, ci, c0, n)
            b.add_item(w_in, 16, [(0 * 512 + h * 128, 128), (1 * 512 + h * 128, 128)], comp1)
            b.add_item(w_in, 16, [(2 * 512 + h * 128, 128), (CC + h * 128, 128)], comp2)
            b.add_item(w_in, 16, [(2048 + h * 128, 128), (2560 + h * 128, 128)], comp3)
        b.flush_items()

        def cm_src(kc, ci, c0, n):
            if kc < 4:
                return om[:, kc, c0:c0 + n], ("om", kc, ci)
            return om[:, kc, c0:c0 + n], ("mp", kc - 4, ci)
        _out_ln(b, l, b_w_out[i_], 8, cm_src, CTB, last_out=(y_o if l == 3 else None))
    b.p.emit(nc, b.st)
    return b


def _fm(a):
    t, f = a.shape
    return np.ascontiguousarray(a.reshape(t, f // 128, 128).transpose(2, 1, 0))


def _unfm(a):
    p, c, t = a.shape
    return np.ascontiguousarray(a.transpose(2, 1, 0).reshape(t, c * 128))


_CACHE = {}


def run_A(inp):
    if "A" not in _CACHE:
        _CACHE["A"] = build_A()
    b = _CACHE["A"]
    xp = inp["x_prompt"]
    xs = inp["x_sample"]
    in_maps = []
    ident = np.eye(128, dtype=np.float32)
    wdw = np.ascontiguousarray(inp["a_w_dw"].reshape(2, KW, NJ, 128).transpose(3, 0, 2, 1))
    cnp = np.stack([inp["a_b_dw"], inp["a_cn_g"], inp["a_cn_b"]], axis=1)
    cnp = np.ascontiguousarray(cnp.reshape(2, 3, NJ, 128).transpose(3, 0, 1, 2))
    lnp = np.stack([inp["ln_g"], inp["ln_b"]], axis=1)
    lnp = np.ascontiguousarray(lnp.reshape(4, 2, 16, 128).transpose(3, 0, 1, 2))
    for r in range(NCORE):
        bp, c = r // 4, r % 4
        t0 = c * CH
        cols = np.zeros((NTA, D), np.float32)
        if c > 0:
            cols[0:HALO] = xp[bp, t0 - HALO:t0]
        cols[HALO:HALO + CH] = xp[bp, t0:t0 + CH]
        cols[HALO + CH:] = xs[r]
        st = inp["state_conv"][:, r]
        convst = np.ascontiguousarray(st.reshape(2, 30, NJ, 128).transpose(3, 0, 2, 1))
        cm = inp["cache_mem_kv"][:, r]
        ckT = np.ascontiguousarray(cm[:, :, 0].transpose(3, 0, 2, 1))
        cv = np.ascontiguousarray(cm[:, :, 1].reshape(4, 2, 128, 512).transpose(2, 0, 1, 3))
        in_maps.append({
            "xT": _fm(cols), "memT": _fm(inp["mem_prompt"][bp]), "identf": ident,
            "hmask": np.full((128, 1), 0.0 if c == 0 else 1.0, np.float32),
            "a_w_in": inp["a_w_in"], "a_w_out": inp["a_w_out"], "w_kv": inp["w_kv_shared"],
            "w_mem_kv": inp["w_mem_kv"], "wdw": wdw, "cnp": cnp, "lnp": lnp,
            "convst": convst, "convst_nat": np.ascontiguousarray(st), "ckT": ckT, "cv": cv,
            "skv0": np.ascontiguousarray(inp["state_kv_g0"][r].reshape(-1, 1024)),
            "skv1": np.ascontiguousarray(inp["state_kv_g1"][r].reshape(-1, 1024)),
            "skv2": np.ascontiguousarray(inp["state_kv_g2"][r].reshape(-1, 1024)),
        })
    res = run_bass_kernel_spmd(b.nc, in_maps, core_ids=list(range(NCORE)))
    return res.results


def _bucket_np(dist):
    import math
    max_exact = 16
    safe = np.maximum(dist, 1).astype(np.float32)
    large = max_exact + (np.log(safe / np.float32(max_exact)) / np.float32(math.log(2048 / max_exact))
                         * np.float32(32 - max_exact)).astype(np.int32)
    large = np.minimum(large, 31)
    return np.where(dist < max_exact, dist, large)


def _onehots():
    ohp = np.zeros((3, 33, 384), np.float32)
    ohs = []
    for g, (w, d) in enumerate(GROUPS):
        for n in range(384):
            j = n - 127
            if 0 <= j <= 128:
                ohp[g, int(_bucket_np(np.array([d * j]))[0]), n] = 1.0
            else:
                ohp[g, 32, n] = 1.0
        o = np.zeros((33, LS[g]), np.float32)
        for n in range(LS[g]):
            dist = n - 8
            if dist >= 0 and dist % d == 0 and dist // d <= w // d:
                o[int(_bucket_np(np.array([dist]))[0]), n] = 1.0
            else:
                o[32, n] = 1.0
        ohs.append(o)
    return ohp, np.concatenate(ohs, axis=1)


def run_B(inp, ra):
    if "B" not in _CACHE:
        _CACHE["B"] = build_B()
    b = _CACHE["B"]
    ident = np.eye(128, dtype=np.float32)
    lnp = np.stack([inp["ln_g"], inp["ln_b"]], axis=1)
    lnp = np.ascontiguousarray(lnp.reshape(4, 2, 16, 128).transpose(3, 0, 1, 2))
    ohp, ohs = _onehots()
    Kfull, Vfull = [], []
    for bp in range(2):
        Kfull.append(np.concatenate([_unfm(ra[bp * 4 + c]["kT_o"][:, :, :CH]) for c in range(4)], axis=0))
        Vfull.append(np.concatenate([ra[bp * 4 + c]["v_o"][:CH] for c in range(4)], axis=0))
    in_maps = []
    for r in range(NCORE):
        bp, c = r // 4, r % 4
        t0 = c * CH
        mk = ra[bp * 4]["memkv_o"]
        mkT = np.ascontiguousarray(mk[2:4, :, 0:512].reshape(2, NMEM, 4, 128).transpose(3, 0, 2, 1))
        mv = np.ascontiguousarray(mk[2:4, :, 512:1024].reshape(2, 2, 128, 512).transpose(2, 0, 1, 3))
        cm = inp["cache_mem_kv"][:, r]
        ckT = np.ascontiguousarray(cm[:, :, 0].transpose(3, 0, 2, 1))
        cv = np.ascontiguousarray(cm[:, :, 1].reshape(4, 2, 128, 512).transpose(2, 0, 1, 3))
        m = {"x2T": ra[r]["x2_o"], "identf": ident, "b_w_in": inp["b_w_in"], "b_w_out": inp["b_w_out"], "lnp": lnp,
             "mkT": mkT, "mv": mv, "ckT": ckT, "cv": cv, "relb": inp["rel_bias"], "ohp": ohp, "ohs": ohs,
             "sKTn": np.ascontiguousarray(ra[r]["kT_o"][:, :, CH:NTB][:, :, ::-1]),
             "sVn": np.ascontiguousarray(ra[r]["v_o"][CH:NTB][::-1])}
        for g, (w, d) in enumerate(GROUPS):
            nk, nch = NK[g], NCH[g]
            tok = t0 + d * (np.arange(nk)[None, :] - 128) + np.arange(d)[:, None]
            valid = tok >= 0
            tk = np.maximum(tok, 0)
            Kg = Kfull[bp][:, g * 512:(g + 1) * 512].reshape(SEQ, 4, 128)
            Vg = Vfull[bp][:, g * 512:(g + 1) * 512]
            perm = np.concatenate([np.arange(min(128, nk - 128 * ch))[::-1] + 128 * ch for ch in range(nch)])
            tk = tk[:, perm]
            valid = valid[:, perm]
            kt = Kg[tk] * valid[:, :, None, None]
            m["KT%d" % g] = np.ascontiguousarray(kt.transpose(3, 2, 0, 1)).astype(np.float32)
            vpad = np.zeros((d, nch * 128, 512), np.float32)
            vpad[:, :nk] = Vg[tk] * valid[:, :, None]
            m["V%d" % g] = np.ascontiguousarray(vpad.reshape(d, nch, 128, 512).transpose(2, 0, 1, 3))
            kv_ = np.zeros((d, nch * 128), np.float32)
            kv_[:, :nk] = valid
            m["kval%d" % g] = np.ascontiguousarray(kv_.reshape(d, nch, 128).transpose(2, 0, 1).reshape(128, d * nch))
            stt = inp["state_kv_g%d" % g][r][::-1]
            m["sKT%d" % g] = np.ascontiguousarray(stt[:, 0].transpose(2, 1, 0))
            m["sV%d" % g] = np.ascontiguousarray(stt[:, 1].reshape(w // 128, 128, 512).transpose(1, 0, 2))
        in_maps.append(m)
    res = run_bass_kernel_spmd(b.nc, in_maps, core_ids=list(range(NCORE)))
    return res.results


def assemble(inp, ra, rb):
    y_p = np.zeros((2, SEQ, D), np.float32)
    y_s = np.zeros((NCORE, NS, D), np.float32)
    for r in range(NCORE):
        bp, c = r // 4, r % 4
        yo = rb[r]["y_o"]
        y_p[bp, c * CH:(c + 1) * CH] = _unfm(yo[:, :, :CH])
        y_s[r] = _unfm(yo[:, :, CH:])
    conv_p = np.stack([np.stack([_unfm(ra[r]["convp_o"][:, l]) for r in (3, 7)]) for l in range(2)])
    conv_s = np.zeros((2, NCORE, 30, CC), np.float32)
    for r in range(NCORE):
        for l in range(2):
            conv_s[l, r, :22] = ra[r]["convss_o"][l]
            conv_s[l, r, 22:] = _unfm(ra[r]["convsn_o"][:, l])
    outs = [y_p, y_s, conv_p, conv_s]
    for g, (w, d) in enumerate(GROUPS):
        rows = min(w, SEQ)
        kp = np.zeros((2, rows, 2, 4, 128), np.float32)
        for bp in range(2):
            kk = np.concatenate([_unfm(ra[bp * 4 + c]["kT_o"][:, g * 4:(g + 1) * 4, :CH]) for c in range(4)], axis=0)
            vv = np.concatenate([ra[bp * 4 + c]["v_o"][:CH, g * 512:(g + 1) * 512] for c in range(4)], axis=0)
            kp[bp, :, 0] = kk[-rows:].reshape(rows, 4, 128)
            kp[bp, :, 1] = vv[-rows:].reshape(rows, 4, 128)
        ks = np.zeros((NCORE, w, 2, 4, 128), np.float32)
        for r in range(NCORE):
            ks[r, :w - NS] = ra[r]["skv_o%d" % g].reshape(w - NS, 2, 4, 128)
            ks[r, w - NS:, 0] = ra[r]["ks_o"][:, g * 512:(g + 1) * 512].reshape(NS, 4, 128)
            ks[r, w - NS:, 1] = ra[r]["v_o"][CH:NTB, g * 512:(g + 1) * 512].reshape(NS, 4, 128)
        outs += [kp, ks]
    mk = np.stack([ra[0]["memkv_o"], ra[4]["memkv_o"]], axis=1).reshape(4, 2, NMEM, 2, 4, 128)
    outs.append(np.ascontiguousarray(mk))
    return tuple(outs)


def kernel(**inp):
    inp = {k: np.asarray(v) for k, v in inp.items()}
    ra = run_A(inp)
    rb = run_B(inp, ra)
    return assemble(inp, ra, rb)
```
